# Optimizing a Trainium2 kernel written in Bass

```python
import jax, jax.numpy as jnp
from jax import lax
import numpy as np

D_MODEL = 2048
BATCH = 8
SEQ = 2048
DEPTH = 1

HEAD_DIM = 128
N_HEADS_TOTAL = D_MODEL // HEAD_DIM
N_FOX_HEADS = N_HEADS_TOTAL // 2
N_SWA_HEADS = N_HEADS_TOTAL - N_FOX_HEADS
N_SWA_KV_HEADS = max(1, N_SWA_HEADS // 4)
SWA_WINDOW = 128
Q_BLOCK = 128
D_FF = 4 * D_MODEL
ROPE_THETA = 10000.0
NORM_EPS = 1e-6
FOX_W = N_FOX_HEADS * HEAD_DIM
SWA_Q_W = N_SWA_HEADS * HEAD_DIM
SWA_KV_W = N_SWA_KV_HEADS * HEAD_DIM
MIX_W = FOX_W + SWA_Q_W
IN_SPLITS = [FOX_W, FOX_W, FOX_W, N_FOX_HEADS, SWA_Q_W, SWA_KV_W, SWA_KV_W]
IN_PROJ_W = sum(IN_SPLITS)
N_MOD = 6

kernel_name = "hymba_fox_swa_sink_hybrid"


def rmsnorm(x, g):
    xf = x.astype(jnp.float32)
    y = xf * lax.rsqrt(jnp.mean(xf * xf, axis=-1, keepdims=True) + NORM_EPS)
    return (y * g.astype(jnp.float32)).astype(x.dtype)


def rope(x, pos):
    d = x.shape[-1]
    half = d // 2
    inv_freq = 1.0 / (ROPE_THETA ** (jnp.arange(half, dtype=jnp.float32) * (2.0 / d)))
    ang = pos.astype(jnp.float32)[:, None] * inv_freq[None, :]
    cos = jnp.cos(ang)[None, :, None, :]
    sin = jnp.sin(ang)[None, :, None, :]
    xf = x.astype(jnp.float32)
    x1, x2 = xf[..., :half], xf[..., half:]
    out = jnp.concatenate([x1 * cos - x2 * sin, x2 * cos + x1 * sin], axis=-1)
    return out.astype(x.dtype)


def forgetting_attention(q, k, v, log_f):
    B, S, H, d = q.shape
    cum = jnp.cumsum(log_f, axis=1).transpose(0, 2, 1)
    scale = d ** -0.5
    tri = jnp.tril(jnp.ones((Q_BLOCK, Q_BLOCK), dtype=bool))
    outs = []
    for i in range(S // Q_BLOCK):
        q0 = i * Q_BLOCK
        end = q0 + Q_BLOCK
        s = jnp.einsum('bqhd,bkhd->bhqk', q[:, q0:end], k[:, :end],
                       preferred_element_type=jnp.float32) * scale
        s = s + cum[:, :, q0:end, None] - cum[:, :, None, :end]
        mask = jnp.concatenate([jnp.ones((Q_BLOCK, q0), dtype=bool), tri], axis=1)
        s = jnp.where(mask[None, None], s, -jnp.inf)
        p = jax.nn.softmax(s, axis=-1)
        outs.append(jnp.einsum('bhqk,bkhd->bqhd', p.astype(v.dtype), v[:, :end]))
    return jnp.concatenate(outs, axis=1)


def sliding_window_sink_attention(q, k, v, sinks):
    B, S, H, d = q.shape
    KVH = k.shape[2]
    G = H // KVH
    nb = S // Q_BLOCK
    scale = d ** -0.5
    pad = ((0, 0), (Q_BLOCK, 0), (0, 0), (0, 0))
    kp = jnp.pad(k, pad).reshape(B, nb + 1, Q_BLOCK, KVH, d)
    vp = jnp.pad(v, pad).reshape(B, nb + 1, Q_BLOCK, KVH, d)
    kb = jnp.concatenate([kp[:, :-1], kp[:, 1:]], axis=2)
    vb = jnp.concatenate([vp[:, :-1], vp[:, 1:]], axis=2)
    qb = q.reshape(B, nb, Q_BLOCK, KVH, G, d)
    s = jnp.einsum('bnqkgd,bnjkd->bnkgqj', qb, kb,
                   preferred_element_type=jnp.float32) * scale
    qi = jnp.arange(Q_BLOCK)[:, None]
    kj = jnp.arange(2 * Q_BLOCK)[None, :]
    diff = qi + Q_BLOCK - kj
    band = (diff >= 0) & (diff < SWA_WINDOW)
    key_idx = jnp.arange(nb)[:, None] * Q_BLOCK + jnp.arange(2 * Q_BLOCK)[None, :] - Q_BLOCK
    valid = key_idx >= 0
    mask = band[None, :, :] & valid[:, None, :]
    s = jnp.where(mask[None, :, None, None], s, -jnp.inf)
    sink = jnp.broadcast_to(sinks.astype(jnp.float32).reshape(KVH, G)[None, None, :, :, None, None],
                            s.shape[:-1] + (1,))
    p = jax.nn.softmax(jnp.concatenate([s, sink], axis=-1), axis=-1)[..., :-1]
    o = jnp.einsum('bnkgqj,bnjkd->bnqkgd', p.astype(v.dtype), vb)
    return o.reshape(B, S, H, d)


def setup_inputs(seed: int = 0) -> dict:
    key = jax.random.key(seed)
    ks = jax.random.split(key, 16)
    f32 = jnp.float32
    D = D_MODEL
    def nrm(k, shape, s):
        return jax.random.normal(k, shape, f32) * s
    return {
        "x": nrm(ks[0], (BATCH, SEQ, D), 1.0),
        "c": nrm(ks[1], (BATCH, D), 1.0),
        "w_mod": nrm(ks[2], (DEPTH, D, N_MOD * D), D ** -0.5),
        "b_mod": nrm(ks[3], (DEPTH, N_MOD * D), 0.02),
        "g_pre_mix": 1.0 + nrm(ks[4], (DEPTH, D), 0.02),
        "g_post_mix": 1.0 + nrm(ks[5], (DEPTH, D), 0.02),
        "w_in": nrm(ks[6], (DEPTH, D, IN_PROJ_W), D ** -0.5),
        "b_forget": jax.random.uniform(ks[7], (DEPTH, N_FOX_HEADS), f32, 1.0, 5.0),
        "swa_sinks": nrm(ks[8], (DEPTH, N_SWA_HEADS), 0.5),
        "w_out": nrm(ks[9], (DEPTH, MIX_W, D), MIX_W ** -0.5),
        "g_pre_mlp": 1.0 + nrm(ks[10], (DEPTH, D), 0.02),
        "g_post_mlp": 1.0 + nrm(ks[11], (DEPTH, D), 0.02),
        "w_up": nrm(ks[12], (DEPTH, D, D_FF), D ** -0.5),
        "w_down": nrm(ks[13], (DEPTH, D_FF, D), D_FF ** -0.5),
    }


def reference(x, c, w_mod, b_mod, g_pre_mix, g_post_mix, w_in, b_forget, swa_sinks,
              w_out, g_pre_mlp, g_post_mlp, w_up, w_down):
    B, S, D = x.shape
    pos = jnp.arange(S)
    split_idx = np.cumsum(IN_SPLITS)[:-1].tolist()
    cond = jax.nn.silu(c)
    for l in range(DEPTH):
        mod = cond @ w_mod[l] + b_mod[l]
        sh_a, sc_a, gt_a, sh_m, sc_m, gt_m = [m[:, None, :] for m in jnp.split(mod, N_MOD, axis=-1)]

        h = rmsnorm(x, g_pre_mix[l]) * (1.0 + sc_a) + sh_a
        proj = h @ w_in[l]
        fq, fk, fv, fg, sq, sk, sv = jnp.split(proj, split_idx, axis=-1)

        log_f = jax.nn.log_sigmoid(fg.astype(jnp.float32) + b_forget[l].astype(jnp.float32))
        fox = forgetting_attention(fq.reshape(B, S, N_FOX_HEADS, HEAD_DIM),
                                   fk.reshape(B, S, N_FOX_HEADS, HEAD_DIM),
                                   fv.reshape(B, S, N_FOX_HEADS, HEAD_DIM), log_f)

        sq = rope(sq.reshape(B, S, N_SWA_HEADS, HEAD_DIM), pos)
        sk = rope(sk.reshape(B, S, N_SWA_KV_HEADS, HEAD_DIM), pos)
        sv = sv.reshape(B, S, N_SWA_KV_HEADS, HEAD_DIM)
        swa = sliding_window_sink_attention(sq, sk, sv, swa_sinks[l])

        mix = jnp.concatenate([fox.reshape(B, S, FOX_W), swa.reshape(B, S, SWA_Q_W)], axis=-1) @ w_out[l]
        x = x + gt_a * rmsnorm(mix, g_post_mix[l])

        h = rmsnorm(x, g_pre_mlp[l]) * (1.0 + sc_m) + sh_m
        y = jnp.square(jax.nn.relu(h @ w_up[l])) @ w_down[l]
        x = x + gt_m * rmsnorm(y, g_post_mlp[l])
    return x
```

```python
import numpy as np
import ml_dtypes
import concourse.bass as bass
import concourse.mybir as mybir
from concourse.bass_utils import run_bass_kernel_spmd

F32 = mybir.dt.float32
BF16 = mybir.dt.bfloat16
AF = mybir.ActivationFunctionType
ALU = mybir.AluOpType

D = 2048
S = 2048
KC = 16
NT = 4
TS = 512
DFF = 8192
NMOD = 6 * D
INW = 4616
EPS = 1e-6
SCALE = 128 ** -0.5
C_FQ, C_FK, C_FV, C_FG, C_SQ, C_SK, C_SV = 0, 1024, 2048, 3072, 3080, 4104, 4360
MASKV = -30000.0

ENG_ATTR = {"pe": "tensor", "act": "scalar", "dve": "vector", "pool": "gpsimd", "sp": "sync"}


class Op:
    __slots__ = ("eng", "fn", "deps", "dsem", "sig", "ncons", "blk", "keep")


class Res:
    def __init__(self):
        self.ws = []
        self.rs = []

    def rd(self):
        return list(self.ws)

    def wr(self):
        return list(self.ws) + list(self.rs)

    def did_read(self, op):
        self.rs.append(op)

    def did_write(self, op, fresh=True):
        if fresh:
            self.ws = [op]
            self.rs = []
        else:
            self.ws.append(op)


class Prog:
    def __init__(self, nc, stack):
        self.nc = nc
        self.stack = stack
        self.ops = []
        self.sem = {}
        self.cnt = {}
        self.waited = {e: {} for e in ENG_ATTR}
        self.blk = 0
        for e in ("pe", "act", "dve", "pool"):
            self._sem("c_" + e)

    def _sem(self, name):
        if name not in self.sem:
            self.sem[name] = self.stack.enter_context(self.nc.semaphore(name))
            self.cnt[name] = 0
        return self.sem[name]

    def add(self, eng, fn, deps=(), dsem=None, keep=False):
        op = Op()
        op.eng = eng
        op.fn = fn
        op.deps = [d for d in deps if d is not None]
        op.dsem = dsem
        op.sig = None
        op.ncons = 0
        op.blk = self.blk
        op.keep = keep
        for d in op.deps:
            d.ncons += 1
        if dsem is not None:
            self._sem(dsem)
        self.ops.append(op)
        return op

    def emit_block(self):
        nc = self.nc
        ops = self.ops
        self.ops = []
        for op in ops:
            if op.dsem is not None:
                self.cnt[op.dsem] += 16
                op.sig = (op.dsem, self.cnt[op.dsem])
            elif op.fn is not None and (op.ncons > 0 or op.keep):
                nm = "c_" + op.eng
                self.cnt[nm] += 1
                op.sig = (nm, self.cnt[nm])
        cur = self.blk
        with nc.Block() as block:
            for eng, attr in ENG_ATTR.items():
                mine = [op for op in ops if op.eng == eng]
                if not mine:
                    continue

                def body(e, mine=mine, eng=eng):
                    waited = self.waited[eng]
                    for op in mine:
                        for d in op.deps:
                            if d.blk != cur and d.dsem is None:
                                continue
                            sname, val = d.sig
                            if waited.get(sname, 0) < val:
                                e.wait_ge(self.sem[sname], val)
                                waited[sname] = val
                        if op.fn is not None:
                            ins = op.fn(e)
                            if op.sig is not None:
                                ins.then_inc(self.sem[op.sig[0]], 16 if op.dsem is not None else 1)

                getattr(block, attr)(body)
        self.blk += 1


def _consts():
    bf = ml_dtypes.bfloat16
    ident = np.eye(128, dtype=np.float32)
    perm = np.zeros((128, 128), np.float32)
    for m in range(128):
        perm[(m + 64) % 128, m] = 1.0
    kk = np.arange(128)[:, None]
    qq = np.arange(128)[None, :]
    maskc = np.where(kk <= qq, 0.0, MASKV).astype(np.float32)
    mask2 = np.where(kk > qq, 0.0, MASKV).astype(np.float32)
    maskswa = np.concatenate([maskc, mask2, maskc, mask2], axis=1)
    sel = np.zeros((128, 8, 128), np.float32)
    for h in range(8):
        for r in (h, 32 + h, 64 + h):
            sel[r, h, :] = 1.0
    half = 64
    inv = (1.0 / (10000.0 ** (np.arange(half, dtype=np.float32) * (2.0 / 128)))).astype(np.float32)
    ang = (np.arange(S, dtype=np.float32)[None, :] * inv[:, None]).astype(np.float32)
    cos = np.cos(ang).astype(np.float32)
    sin = np.sin(ang).astype(np.float32)
    cosT = np.concatenate([cos, cos], axis=0)
    sinT = np.concatenate([-sin, sin], axis=0)
    return {
        "k_ident": ident.astype(bf), "k_perm": perm.astype(bf), "k_maskc": maskc.astype(bf),
        "k_maskswa": maskswa.astype(bf), "k_sel": sel.astype(bf), "k_cos": cosT, "k_sin": sinT,
        "k_identf": np.eye(8, dtype=np.float32),
    }


OPTS = {"nfox": 8, "mod_bg": True, "swa": True, "nswa": 2, "swa_lvl": 9, "rope": 9}


def build(stop="full", debug=()):
    nc = bass.Bass("TRN2", target_bir_lowering=False)
    from contextlib import ExitStack

    def din(name, shape, dt=F32):
        return nc.dram_tensor(name, list(shape), dt, kind="ExternalInput").ap()

    xT = din("xT", [D, S])
    c_col = din("c_col", [128, KC])
    w_mod = din("w_mod", [D, NMOD])
    b_modT = din("b_modT", [128, 96])
    gvec = din("gvec", [128, 4, KC])
    w_in = din("w_in", [D, INW])
    b_fg = din("b_fg", [8, 1])
    sinksB = din("sinksB", [128, 8])
    w_out = din("w_out", [D, D])
    w_up = din("w_up", [D, DFF])
    w_down = din("w_down", [DFF, D])
    k_ident = din("k_ident", [128, 128], BF16)
    k_perm = din("k_perm", [128, 128], BF16)
    k_maskc = din("k_maskc", [128, 128], BF16)
    k_maskswa = din("k_maskswa", [128, 512], BF16)
    k_sel = din("k_sel", [128, 8, 128], BF16)
    k_cos = din("k_cos", [128, S])
    k_sin = din("k_sin", [128, S])
    k_identf = din("k_identf", [8, 8])
    outT = nc.dram_tensor("outT", [D, S], F32, kind="ExternalOutput").ap()
    dbg = {}
    for nm, shape, dt in debug:
        dbg[nm] = nc.dram_tensor(nm, list(shape), dt, kind="ExternalOutput").ap()

    xT_v = xT.rearrange("(k p) s -> p k s", p=128)
    outT_v = outT.rearrange("(k p) s -> p k s", p=128)
    w_mod_v = w_mod.rearrange("(k p) n -> p k n", p=128)
    w_in_v = w_in.rearrange("(k p) n -> p k n", p=128)
    w_out_v = w_out.rearrange("(k p) n -> p k n", p=128)
    w_up_v = w_up.rearrange("(k p) n -> p k n", p=128)
    w_down_v = w_down.rearrange("(k p) n -> p k n", p=128)
    wo_b = nc.dram_tensor("wo_b", [8, 128, 16, 256], BF16).ap()
    wu_b = nc.dram_tensor("wu_b", [32, 128, 16, 256], BF16).ap()
    wd_b = nc.dram_tensor("wd_b", [32, 128, 8, 512], BF16).ap()
    conv_jobs = []
    for oc in range(8):
        conv_jobs.append((wo_b[oc], w_out_v[:, :, oc * 256:(oc + 1) * 256]))
    for g in range(8):
        for ut in range(4):
            c0 = g * 1024 + ut * 256
            conv_jobs.append((wu_b[g * 4 + ut], w_up_v[:, :, c0:c0 + 256]))
        for dt in range(4):
            conv_jobs.append((wd_b[g * 4 + dt], w_down_v[:, g * 8:(g + 1) * 8, dt * 512:(dt + 1) * 512]))
    conv_ops = []

    with ExitStack() as top:
        P = Prog(nc, top)

        def sb(stack, name, shape, dt):
            return stack.enter_context(nc.sbuf_tensor(name, list(shape), dt))

        banks = [top.enter_context(nc.psum_tensor(f"bank{i}", [128, 512], F32)) for i in range(8)]
        bres = [Res() for _ in range(8)]
        mixT = sb(top, "mixT", [128, KC * S], BF16)
        mix_v = mixT[:].rearrange("p (k s) -> p k s", k=KC)
        mix_res = Res()
        ones_bf = sb(top, "ones_bf", [128, 128], BF16)
        ident_bf = sb(top, "ident_bf", [128, 128], BF16)
        modT = sb(top, "modT", [128, 96], F32)
        bmod_sb = sb(top, "bmod_sb", [128, 96], F32)
        gv = sb(top, "gv", [128, 4, KC], F32)
        A1 = sb(top, "A1", [128, KC], F32)
        G1 = sb(top, "G1", [128, KC], F32)
        A2 = sb(top, "A2", [128, KC], F32)
        G2 = sb(top, "G2", [128, KC], F32)
        cc = sb(top, "cc", [128, KC], F32)
        cond_bf = sb(top, "cond_bf", [128, KC], BF16)

        final_deps = []
        CT = {k: 0 for k in ['pb_ctr', 'sbank_ctr', 'jt_ctr', 'ps_ctr', 'rp_ctr', 'qblk_ctr', 'wrot_ctr', 'cs_ctr', 'ring_ctr', 'ob_ctr', 'sq_ctr', 'rt_ctr', 'mod_i']}

        def dma(eng, out, in_, deps, dsem):
            return P.add(eng, lambda e: e.dma_start(out=out, in_=in_), deps, dsem=dsem)

        NCVA = 48

        def issue_conv(n):
            for _ in range(n):
                if conv_jobs:
                    dst, srcap = conv_jobs.pop(0)
                    conv_ops.append(dma("pool", dst, srcap, [], "cvA" if len(conv_ops) < NCVA else "cvB"))

        def dump(name, src_ap, deps):
            if name in dbg:
                op = dma("sp", dbg[name], src_ap, deps, dsem="dbg_" + name)
                final_deps.append(op)

        ld_c = dma("sp", cc[:], c_col, [], "ld_small0")
        ld_bm = dma("sp", bmod_sb[:], b_modT, [], "ld_small1")
        ld_gv = dma("sp", gv[:], gvec, [], "ld_small2")
        ld_id = dma("sp", ident_bf[:], k_ident, [], "ld_small3")
        op_ones = P.add("dve", lambda e: e.memset(ones_bf[:], 1.0), [], keep=True)
        op_cond = P.add("act", lambda e: e.activation(out=cond_bf[:], in_=cc[:], func=AF.Silu), [ld_c], keep=True)

        MB = 7

        def mod_dma(wm, wm_res, slot, c0, ncols):
            w = wm[slot]
            r = wm_res[slot]
            ld = dma("pool", w[:, :, 0:ncols], w_mod_v[:, :, c0:c0 + ncols], r.wr(), f"wm{slot}")
            r.did_write(ld)
            return ld

        def mod_pe(wm, wm_res, slot, c0, ncols):
            w = wm[slot]
            r = wm_res[slot]

            def fn(e):
                ins = None
                for jj in range(ncols // 128):
                    j = c0 // 128 + jj
                    for k in range(KC):
                        ins = e.matmul(banks[MB][:, j:j + 1], lhsT=w[:, k, jj * 128:(jj + 1) * 128],
                                       rhs=cond_bf[:, k:k + 1], start=(k == 0), stop=(k == KC - 1))
                return ins

            mm = P.add("pe", fn, r.rd() + [op_cond] + bres[MB].wr())
            r.did_read(mm)
            bres[MB].did_write(mm, fresh=False)
            return mm

        def mod_tile(wm, wm_res, slot, c0, ncols):
            mod_dma(wm, wm_res, slot, c0, ncols)
            return mod_pe(wm, wm_res, slot, c0, ncols)

        def mod_finish(j0, j1, deps):
            op = P.add("dve", lambda e: e.tensor_tensor(out=modT[:, j0:j1], in0=banks[MB][:, j0:j1],
                                                        in1=bmod_sb[:, j0:j1], op=ALU.add),
                       deps + [ld_bm], keep=True)
            bres[MB].did_read(op)
            return op

        with ExitStack() as sc_h:
            hT = sb(sc_h, "hT", [128, KC, S], BF16)
            hT_res = Res()
            with ExitStack() as sc_n:
                wm = [sb(sc_n, f"wmN{i}", [128, KC, 512], BF16) for i in range(2)]
                wm_res = [Res(), Res()]
                sq = sb(sc_n, "sqN", [128, KC, TS], BF16)
                sq_res = Res()
                rstd = [sb(sc_n, f"rstdN{i}", [128, TS], F32) for i in range(2)]
                rstd_res = [Res(), Res()]
                xs_all = mixT[:].bitcast(F32).rearrange("p (i k s) -> p i k s", i=2, k=KC)
                xs_res = [Res(), Res()]

                def norm_stage1(t):
                    i = t % 2
                    xs = xs_all[:, i]
                    ld = dma("sp", xs, xT_v[:, :, t * TS:(t + 1) * TS], xs_res[i].wr(), f"xs{i}")
                    xs_res[i].did_write(ld)
                    o_sq = P.add("act", lambda e, xs=xs: e.activation(out=sq[:], in_=xs, func=AF.Square),
                                 xs_res[i].rd() + sq_res.wr())
                    xs_res[i].did_read(o_sq)
                    sq_res.did_write(o_sq)
                    bk = t % 2

                    def fn_ss(e, bk=bk):
                        ins = None
                        for k in range(KC):
                            ins = e.matmul(banks[bk][:, :], lhsT=ones_bf[:], rhs=sq[:, k, :],
                                           start=(k == 0), stop=(k == KC - 1))
                        return ins

                    o_ss = P.add("pe", fn_ss, sq_res.rd() + bres[bk].wr() + [op_ones])
                    sq_res.did_read(o_ss)
                    bres[bk].did_write(o_ss)
                    rs = rstd[i]
                    o_sqrt = P.add("act", lambda e, rs=rs, bk=bk: e.activation(out=rs[:], in_=banks[bk][:, :], func=AF.Sqrt,
                                                                                 bias=EPS, scale=1.0 / D),
                                   bres[bk].rd() + rstd_res[i].wr())
                    bres[bk].did_read(o_sqrt)
                    rstd_res[i].did_write(o_sqrt)
                    o_rec = P.add("dve", lambda e, rs=rs: e.reciprocal(out=rs[:], in_=rs[:]), rstd_res[i].rd())
                    rstd_res[i].did_write(o_rec)
                    o_mul = P.add("dve", lambda e, xs=xs, rs=rs: e.tensor_tensor(
                        out=xs, in0=xs, in1=rs[:].unsqueeze(1).broadcast_to([128, KC, TS]), op=ALU.mult),
                        rstd_res[i].rd() + xs_res[i].wr())
                    rstd_res[i].did_read(o_mul)
                    xs_res[i].did_write(o_mul)

                def norm_stage2(t, opA1):
                    i = t % 2
                    xs = xs_all[:, i]

                    def fn_h(e, xs=xs, t=t):
                        ins = None
                        for k in range(KC):
                            ins = e.activation(out=hT[:, k, t * TS:(t + 1) * TS], in_=xs[:, k, :], func=AF.Identity,
                                               bias=modT[:, k:k + 1], scale=A1[:, k:k + 1])
                        return ins

                    o_h = P.add("act", fn_h, xs_res[i].rd() + [opA1] + hT_res.wr(), keep=True)
                    xs_res[i].did_read(o_h)
                    hT_res.did_write(o_h, fresh=(t == 0))

                norm_stage1(0)
                norm_stage1(1)
                mms = [mod_tile(wm, wm_res, jt % 2, jt * 512, 512) for jt in range(8)]
                fin1 = mod_finish(0, 32, [mms[-1]])
                opA1 = P.add("dve", lambda e: e.scalar_tensor_tensor(out=A1[:], in0=modT[:, 16:32], scalar=1.0,
                                                                    in1=gv[:, 0, :], op0=ALU.add, op1=ALU.mult),
                             [fin1, ld_gv], keep=True)
                norm_stage2(0, opA1)
                norm_stage2(1, opA1)
                norm_stage1(2)
                norm_stage1(3)
                norm_stage2(2, opA1)
                norm_stage2(3, opA1)
                if "d_hT" in dbg:
                    dump("d_hT", hT[:], hT_res.rd())
                    dump("d_modT", modT[:], [fin1])
                P.emit_block()
            if stop == "N":
                fin = P.add("sp", None, final_deps)
                P.emit_block()
                return nc

            CQ = sb(sc_h, "CQ", [128, S], BF16)
            Ctok = sb(sc_h, "Ctok", [128, 128], F32)
            sel_bf = sb(sc_h, "sel_bf", [128, 8, 128], BF16)
            maskc_bf = sb(sc_h, "maskc_bf", [128, 128], BF16)
            ld_sel = dma("sp", sel_bf[:], k_sel, [], "ld_small0")
            ld_mc = dma("sp", maskc_bf[:], k_maskc, [], "ld_small1")

            with ExitStack() as sc_f:
                wg3 = sb(sc_f, "wg3", [128, KC, 72], BF16)
                b72 = sb(sc_f, "b72", [72, 1], F32)
                nb72 = sb(sc_f, "nb72", [72, 1], F32)
                Cf = sb(sc_f, "Cf", [72, S], F32)
                lf = sb(sc_f, "lf", [72, S], F32)
                onesF = sb(sc_f, "onesF", [72, S], F32)
                Thi = sb(sc_f, "Thi", [72, S], BF16)
                identF = sb(sc_f, "identF", [8, 8], F32)
                ld_if = dma("sp", identF[:], k_identf, [], "ld_small2")
                ms_w = P.add("dve", lambda e: e.memset(wg3[:], 0.0), [])
                ms_b = P.add("dve", lambda e: e.memset(b72[:], 0.0), [])
                ms_1 = P.add("dve", lambda e: e.memset(onesF[:], 1.0), [])
                ms_q = P.add("dve", lambda e: e.memset(CQ[:], 0.0), [])
                ldw = []
                ldb = []
                for i, r0 in enumerate((0, 32, 64)):
                    ldw.append(dma("pool", wg3[:, :, r0:r0 + 8], w_in_v[:, :, C_FG:C_FG + 8], [ms_w], f"wg{i}"))
                    ldb.append(dma("sp", b72[r0:r0 + 8, :], b_fg, [ms_b], f"bg{i}"))
                o_nb = P.add("dve", lambda e: e.tensor_scalar(out=nb72[:], in0=b72[:], scalar1=-1.0, scalar2=None,
                                                              op0=ALU.mult), ldb)
                lf_w = []
                for t in range(NT):
                    bk = t % 2

                    def fn_fg(e, t=t, bk=bk):
                        ins = None
                        for k in range(KC):
                            ins = e.matmul(banks[bk][0:72, :], lhsT=wg3[:, k, :], rhs=hT[:, k, t * TS:(t + 1) * TS],
                                           start=(k == 0), stop=(k == KC - 1))
                        return ins

                    o_fg = P.add("pe", fn_fg, ldw + hT_res.rd() + bres[bk].wr())
                    bres[bk].did_write(o_fg)
                    o_e = P.add("act", lambda e, t=t, bk=bk: e.activation(
                        out=lf[:, t * TS:(t + 1) * TS], in_=banks[bk][0:72, :], func=AF.Exp, bias=nb72[:, 0:1], scale=-1.0),
                        bres[bk].rd() + [o_nb])
                    bres[bk].did_read(o_e)
                    o_l = P.add("act", lambda e, t=t: e.activation(
                        out=lf[:, t * TS:(t + 1) * TS], in_=lf[:, t * TS:(t + 1) * TS], func=AF.Ln, bias=1.0, scale=1.0),
                        [o_e])
                    lf_w.append(o_l)
                o_scan = P.add("dve", lambda e: e.tensor_tensor_scan(out=Cf[:], data0=onesF[:], data1=lf[:], initial=0.0,
                                                                     op0=ALU.mult, op1=ALU.add), lf_w + [ms_1])
                o_neg = P.add("dve", lambda e: e.tensor_scalar(out=lf[:], in0=Cf[:], scalar1=-1.0, scalar2=None,
                                                               op0=ALU.mult), [o_scan])
                o_hi = P.add("dve", lambda e: e.tensor_copy(out=Thi[:], in_=lf[:]), [o_neg])
                o_r1 = P.add("dve", lambda e: e.tensor_tensor(out=onesF[:], in0=lf[:], in1=Thi[:], op=ALU.subtract), [o_hi])
                o_mid = P.add("dve", lambda e: e.tensor_copy(out=CQ[0:72, :], in_=onesF[:]), [o_r1, ms_q])
                o_r2 = P.add("dve", lambda e: e.tensor_tensor(out=lf[:], in0=onesF[:], in1=CQ[0:72, :], op=ALU.subtract), [o_mid])
                o_lo = P.add("dve", lambda e: e.tensor_copy(out=CQ[64:72, :], in_=lf[64:72, :]), [o_r2])
                o_hi2 = P.add("dve", lambda e: e.tensor_copy(out=CQ[0:32, :], in_=Thi[0:32, :]), [o_r2], keep=True)
                cq_ready = [o_lo, o_hi2]
                o_lo.keep = True

                def fn_tr(e):
                    ins = None
                    for blk in range(16):
                        ins = e.transpose(out=banks[2][:, blk * 8:(blk + 1) * 8], in_=Cf[0:8, blk * 128:(blk + 1) * 128],
                                          identity=identF[:])
                    return ins

                o_tr = P.add("pe", fn_tr, [o_scan, ld_if] + bres[2].wr())
                bres[2].did_write(o_tr)
                o_ct = P.add("dve", lambda e: e.tensor_copy(out=Ctok[:], in_=banks[2][:, 0:128]), [o_tr], keep=True)
                bres[2].did_read(o_ct)
                if "d_C" in dbg:
                    dump("d_C", Cf[:], [o_scan])
                    dump("d_Ctok", Ctok[:], [o_ct])
                    dump("d_CQ", CQ[:], cq_ready)
                P.emit_block()
            if stop == "FG":
                P.add("sp", None, final_deps)
                P.emit_block()
                return nc

            pbank_ctr = [0]

            def next_pbank():
                b = pbank_ctr[0] % 3
                pbank_ctr[0] += 1
                return b

            def proj_fm(wt, wres, t, evac_fn, evac_eng, dst_res, fresh):
                bk = next_pbank()

                def fn(e, bk=bk):
                    ins = None
                    for k in range(KC):
                        ins = e.matmul(banks[bk][:, :], lhsT=wt[:, k, :], rhs=hT[:, k, t * TS:(t + 1) * TS],
                                       start=(k == 0), stop=(k == KC - 1))
                    return ins

                o = P.add("pe", fn, wres.rd() + hT_res.rd() + bres[bk].wr())
                wres.did_read(o)
                bres[bk].did_write(o)
                ev = P.add(evac_eng, lambda e, bk=bk: evac_fn(e, banks[bk]), bres[bk].rd() + dst_res.wr() if fresh else bres[bk].rd() + dst_res.rs)
                bres[bk].did_read(ev)
                dst_res.did_write(ev, fresh=fresh)
                return o, ev, bk

            def proj_v_tok(wt, wres, blk4, vt, vres, fresh):
                bk = next_pbank()

                def fn(e, bk=bk):
                    ins = None
                    for bi in range(4):
                        blk = blk4 * 4 + bi
                        for k in range(KC):
                            ins = e.matmul(banks[bk][:, bi * 128:(bi + 1) * 128], lhsT=hT[:, k, blk * 128:(blk + 1) * 128],
                                           rhs=wt[:, k, :], start=(k == 0), stop=(k == KC - 1))
                    return ins

                o = P.add("pe", fn, wres.rd() + hT_res.rd() + bres[bk].wr())
                wres.did_read(o)
                bres[bk].did_write(o)
                ev = P.add("dve", lambda e, bk=bk: e.tensor_copy(
                    out=vt[:, blk4 * 4:(blk4 + 1) * 4, :], in_=banks[bk][:, :].rearrange("p (a d) -> p a d", a=4)),
                    bres[bk].rd() + (vres.wr() if fresh else vres.rs))
                bres[bk].did_read(ev)
                vres.did_write(ev, fresh=fresh)
                return ev

            with ExitStack() as sc_x:
                wm2 = [sb(sc_x, f"wmX{i}", [128, KC, 128], BF16) for i in range(2)]
                wm2_res = [Res(), Res()]
                wq = [sb(sc_x, f"wq{i}", [128, KC, 128], BF16) for i in range(2)]
                wk = [sb(sc_x, f"wk{i}", [128, KC, 128], BF16) for i in range(2)]
                wv = [sb(sc_x, f"wv{i}", [128, KC, 128], BF16) for i in range(2)]
                wq_res = [Res(), Res()]
                wk_res = [Res(), Res()]
                wv_res = [Res(), Res()]
                qT = [sb(sc_x, f"qT{i}", [128, S], BF16) for i in range(2)]
                kT = [sb(sc_x, f"kT{i}", [128, S], BF16) for i in range(2)]
                vtok = [sb(sc_x, f"vtok{i}", [128, 16, 128], BF16) for i in range(2)]
                q_res = [Res(), Res()]
                k_res = [Res(), Res()]
                v_res = [Res(), Res()]
                NPB = 4
                Pb = [sb(sc_x, f"Pb{i}", [128, TS], BF16) for i in range(NPB)]
                Pb_res = [Res() for _ in range(NPB)]
                rden = [sb(sc_x, f"rden{i}", [128, TS], F32) for i in range(2)]
                rden_res = [Res(), Res()]
                CT["pb_ctr"] = 0
                CT["sbank_ctr"] = 0
                mod_jobs = [(32 * 128 + i * 128) for i in range(64)]
                mod_ops = []
                CT["mod_i"] = 0
                CT["jt_ctr"] = 0
                def load_head(hh):
                    ss = hh % 2
                    for (wt, wres, c0, nm) in ((wq, wq_res, C_FQ, "wq"), (wk, wk_res, C_FK, "wk"), (wv, wv_res, C_FV, "wv")):
                        ld = dma("pool", wt[ss][:], w_in_v[:, :, c0 + hh * 128:c0 + (hh + 1) * 128], wres[ss].wr(), f"{nm}{ss}")
                        wres[ss].did_write(ld)

                def mod_step():
                    i = CT["mod_i"]
                    if not OPTS["mod_bg"] or i >= len(mod_jobs):
                        return
                    mod_ops.append(mod_pe(wm2, wm2_res, i % 2, mod_jobs[i], 128))
                    if i + 2 < len(mod_jobs):
                        mod_dma(wm2, wm2_res, i % 2, mod_jobs[i + 2], 128)
                    CT["mod_i"] += 1

                if OPTS["nfox"] > 0:
                    load_head(0)
                if OPTS["mod_bg"]:
                    mod_dma(wm2, wm2_res, 0, mod_jobs[0], 128)
                    mod_dma(wm2, wm2_res, 1, mod_jobs[1], 128)

                def fox_head(h):
                    s = h % 2
                    if h + 1 < OPTS["nfox"]:
                        load_head(h + 1)
                    for t in range(NT):
                        proj_fm(wq[s], wq_res[s], t,
                                lambda e, bank, t=t, s=s: e.activation(out=qT[s][:, t * TS:(t + 1) * TS], in_=bank[:, :],
                                                                        func=AF.Copy, scale=SCALE),
                                "act", q_res[s], fresh=(t == 0))
                        mod_step()
                    for t in range(NT):
                        proj_fm(wk[s], wk_res[s], t,
                                lambda e, bank, t=t, s=s: e.tensor_copy(out=kT[s][:, t * TS:(t + 1) * TS], in_=bank[:, :]),
                                "dve", k_res[s], fresh=(t == 0))
                        mod_step()
                    for b4 in range(4):
                        proj_v_tok(wv[s], wv_res[s], b4, vtok[s], v_res[s], fresh=(b4 == 0))
                    issue_conv(6)
                    def fox_qtile(j):
                        ob = 3 + 2 * (CT["jt_ctr"] % 2)
                        db = ob + 1
                        rd_i = CT["jt_ctr"] % 2
                        CT["jt_ctr"] += 1
                        nkb = 4 * j + 4
                        qk_ops = {}
                        ex_ops = {}

                        def make_qk(kb):
                            sbk = CT["sbank_ctr"] % 3
                            CT["sbank_ctr"] += 1
                            c0 = max(0, kb - 4 * j) * 128

                            def fn(e, kb=kb, sbk=sbk, c0=c0):
                                e.matmul(banks[sbk][:, c0:TS], lhsT=kT[s][:, kb * 128:(kb + 1) * 128],
                                         rhs=qT[s][:, j * TS + c0:(j + 1) * TS], start=True, stop=False)
                                diag = kb >= 4 * j
                                ins = e.matmul(banks[sbk][:, c0:TS], lhsT=sel_bf[0:72, h, :],
                                               rhs=CQ[0:72, j * TS + c0:(j + 1) * TS], start=False, stop=not diag)
                                if diag:
                                    ins = e.matmul(banks[sbk][:, c0:c0 + 128], lhsT=ident_bf[:], rhs=maskc_bf[:],
                                                   start=False, stop=True)
                                return ins

                            o = P.add("pe", fn, k_res[s].rd() + q_res[s].rd() + cq_ready + [ld_sel, ld_mc, ld_id] + bres[sbk].wr())
                            k_res[s].did_read(o)
                            q_res[s].did_read(o)
                            bres[sbk].did_write(o)
                            qk_ops[kb] = (o, sbk, c0)

                        def make_exp(kb):
                            o_qk, sbk, c0 = qk_ops[kb]
                            pi = CT["pb_ctr"] % NPB
                            CT["pb_ctr"] += 1
                            o = P.add("act", lambda e, sbk=sbk, c0=c0, pi=pi, kb=kb: e.activation(
                                out=Pb[pi][:, c0:TS], in_=banks[sbk][:, c0:TS], func=AF.Exp,
                                bias=Ctok[:, kb * 8 + h:kb * 8 + h + 1], scale=1.0),
                                bres[sbk].rd() + Pb_res[pi].wr() + [o_ct])
                            bres[sbk].did_read(o)
                            Pb_res[pi].did_write(o)
                            ex_ops[kb] = (o, pi, c0)

                        def make_pv(kb):
                            o_ex, pi, c0 = ex_ops[kb]

                            def fn(e, kb=kb, pi=pi, c0=c0):
                                e.matmul(banks[ob][:, c0:TS], lhsT=vtok[s][:, kb, :], rhs=Pb[pi][:, c0:TS],
                                         start=(kb == 0), stop=(kb == nkb - 1))
                                return e.matmul(banks[db][:, c0:TS], lhsT=ones_bf[:], rhs=Pb[pi][:, c0:TS],
                                                start=(kb == 0), stop=(kb == nkb - 1))

                            deps = Pb_res[pi].rd() + v_res[s].rd()
                            if kb == 0:
                                deps = deps + bres[ob].wr() + bres[db].wr()
                            o = P.add("pe", fn, deps)
                            Pb_res[pi].did_read(o)
                            v_res[s].did_read(o)
                            bres[ob].did_write(o, fresh=(kb == 0))
                            bres[db].did_write(o, fresh=(kb == 0))
                            return o

                        make_qk(0)
                        if nkb > 1:
                            make_qk(1)
                        last_pv = None
                        for kb in range(nkb):
                            make_exp(kb)
                            last_pv = make_pv(kb)
                            if kb + 2 < nkb:
                                make_qk(kb + 2)
                        rd = rden[rd_i]
                        o_rc = P.add("dve", lambda e, rd=rd, db=db: e.reciprocal(out=rd[:], in_=banks[db][:, :]),
                                     [last_pv] + rden_res[rd_i].wr())
                        bres[db].did_read(o_rc)
                        rden_res[rd_i].did_write(o_rc)
                        o_mx = P.add("dve", lambda e, rd=rd, ob=ob, j=j: e.tensor_tensor(
                            out=mix_v[:, h, j * TS:(j + 1) * TS], in0=banks[ob][:, :], in1=rd[:], op=ALU.mult),
                            [last_pv, o_rc] + mix_res.rs, keep=True)
                        bres[ob].did_read(o_mx)
                        rden_res[rd_i].did_read(o_mx)
                        mix_res.did_write(o_mx, fresh=False)
                    for j in range(NT):
                        fox_qtile(j)

                for h in range(OPTS["nfox"]):
                    fox_head(h)
                fin2 = mod_finish(32, 32 + CT["mod_i"], mod_ops[-1:]) if CT["mod_i"] > 0 else None
                oG1 = oA2 = oG2 = None
                if fin2 is not None:
                    oG1 = P.add("dve", lambda e: e.tensor_tensor(out=G1[:], in0=modT[:, 32:48], in1=gv[:, 1, :], op=ALU.mult), [fin2], keep=True)
                    oA2 = P.add("dve", lambda e: e.scalar_tensor_tensor(out=A2[:], in0=modT[:, 64:80], scalar=1.0, in1=gv[:, 2, :],
                                                                        op0=ALU.add, op1=ALU.mult), [fin2], keep=True)
                    oG2 = P.add("dve", lambda e: e.tensor_tensor(out=G2[:], in0=modT[:, 80:96], in1=gv[:, 3, :], op=ALU.mult), [fin2], keep=True)
                P.emit_block()

            with ExitStack() as sc_s:
                wrot = [sb(sc_s, f"wrot{i}", [128, KC, 128], BF16) for i in range(2)]
                wrot_res = [Res(), Res()]
                wsv = sb(sc_s, "wsv", [128, KC, 128], BF16)
                wsv_res = Res()
                sqT = sb(sc_s, "sqT", [128, 4, S], BF16)
                skT = sb(sc_s, "skT", [128, S], BF16)
                vts = sb(sc_s, "vts", [128, 16, 128], BF16)
                sq_res = Res()
                sk_res = Res()
                vs_res = Res()
                cs = [sb(sc_s, f"cs{i}", [128, TS], F32) for i in range(2)]
                sn = [sb(sc_s, f"sn{i}", [128, TS], F32) for i in range(2)]
                cs_res = [Res(), Res()]
                qb = [sb(sc_s, f"qb{i}", [128, TS], BF16) for i in range(2)]
                qb_res = [Res(), Res()]
                t1 = [sb(sc_s, f"t1_{i}", [128, TS], F32) for i in range(1)]
                t2 = [sb(sc_s, f"t2_{i}", [128, TS], F32) for i in range(1)]
                t_res = [Res()]
                NPS = 6
                Ps = [sb(sc_s, f"Ps{i}", [128, TS], BF16) for i in range(NPS)]
                Ps_res = [Res() for _ in range(NPS)]
                perm_bf = sb(sc_s, "perm_bf", [128, 128], BF16)
                mswa_bf = sb(sc_s, "mswa_bf", [128, 512], BF16)
                sinks_sb = sb(sc_s, "sinks_sb", [128, 8], F32)
                esink = sb(sc_s, "esink", [128, 8], F32)
                esinkB = sb(sc_s, "esinkB", [128, 8, 128], F32)
                rds = [sb(sc_s, f"rds{i}", [128, TS], F32) for i in range(2)]
                rds_res = [Res(), Res()]
                ld_pm = dma("sp", perm_bf[:], k_perm, [], "ld_small0")
                ld_ms = dma("sp", mswa_bf[:], k_maskswa, [], "ld_small1")
                ld_sk = dma("sp", sinks_sb[:], sinksB, [], "ld_small2")
                o_es = P.add("act", lambda e: e.activation(out=esink[:], in_=sinks_sb[:], func=AF.Exp), [ld_sk])

                def fn_esb(e):
                    ins = None
                    for hh in range(8):
                        ins = e.activation(out=esinkB[:, hh, :], in_=ident_bf[:], func=AF.Identity,
                                           bias=esink[:, hh:hh + 1], scale=0.0)
                    return ins

                o_esb = P.add("act", fn_esb, [o_es, ld_id], keep=True)
                CT["ps_ctr"] = 0
                CT["rp_ctr"] = 0
                CT["qblk_ctr"] = 0
                CT["wrot_ctr"] = 0
                CT["cs_ctr"] = 0
                def swa_group(g):
                    ld = dma("pool", wsv[:], w_in_v[:, :, C_SV + g * 128:C_SV + (g + 1) * 128], wsv_res.wr(), "wsv")
                    wsv_res.did_write(ld)
                    def load_rot(gg, pp):
                        wi = (gg * 5 + pp) % 2
                        if pp < 4:
                            c0w = C_SQ + (4 * gg + pp) * 128
                        else:
                            c0w = C_SK + gg * 128
                        ld = dma("pool", wrot[wi][:], w_in_v[:, :, c0w:c0w + 128], wrot_res[wi].wr(), f"wrot{wi}")
                        wrot_res[wi].did_write(ld)

                    if g == 0:
                        load_rot(0, 0)
                    for pi5 in range(5):
                        wi = (g * 5 + pi5) % 2
                        wt, wres = wrot[wi], wrot_res[wi]
                        nxt = g * 5 + pi5 + 1
                        if nxt < 5 * (OPTS["nswa"] if OPTS["swa"] else 0):
                            load_rot(nxt // 5, nxt % 5)
                        for t in range(NT):
                            ci = CT["cs_ctr"] % 2
                            CT["cs_ctr"] += 1
                            ldc = dma("sp", cs[ci][:], k_cos[:, t * TS:(t + 1) * TS], cs_res[ci].wr(), f"xs{ci}")
                            lds = dma("sp", sn[ci][:], k_sin[:, t * TS:(t + 1) * TS], cs_res[ci].wr(), f"wm{ci}")
                            cs_res[ci].did_write(ldc)
                            cs_res[ci].did_write(lds, fresh=False)
                            ri = CT["rp_ctr"] % 2
                            CT["rp_ctr"] += 1
                            bkA = next_pbank()

                            def fn_p(e, wt=wt, bkA=bkA, t=t):
                                ins = None
                                for k in range(KC):
                                    ins = e.matmul(banks[bkA][:, :], lhsT=wt[:, k, :], rhs=hT[:, k, t * TS:(t + 1) * TS],
                                                   start=(k == 0), stop=(k == KC - 1))
                                return ins

                            o_p = P.add("pe", fn_p, wres.rd() + hT_res.rd() + bres[bkA].wr())
                            wres.did_read(o_p)
                            bres[bkA].did_write(o_p)
                            if pi5 < 4:
                                dst = sqT[:, pi5, t * TS:(t + 1) * TS]
                                dres = sq_res
                            else:
                                dst = skT[:, t * TS:(t + 1) * TS]
                                dres = sk_res
                            fresh = (t == 0 and pi5 in (0, 4))
                            if OPTS["rope"] == 0:
                                o_cp = P.add("dve", lambda e, bkA=bkA, dst=dst: e.tensor_copy(out=dst, in_=banks[bkA][:, :]),
                                             bres[bkA].rd() + (dres.wr() if fresh else dres.rs))
                                bres[bkA].did_read(o_cp)
                                dres.did_write(o_cp, fresh=fresh)
                                continue
                            o_qb = P.add("act", lambda e, ri=ri, bkA=bkA: e.activation(out=qb[ri][:], in_=banks[bkA][:, :], func=AF.Copy),
                                         bres[bkA].rd() + qb_res[ri].wr())
                            bres[bkA].did_read(o_qb)
                            qb_res[ri].did_write(o_qb)
                            bkB = next_pbank()
                            o_pm = P.add("pe", lambda e, ri=ri, bkB=bkB: e.matmul(banks[bkB][:, :], lhsT=perm_bf[:], rhs=qb[ri][:],
                                                                                    start=True, stop=True),
                                         qb_res[ri].rd() + bres[bkB].wr() + [ld_pm])
                            qb_res[ri].did_read(o_pm)
                            bres[bkB].did_write(o_pm)
                            if OPTS["rope"] == 1:
                                o_cp = P.add("dve", lambda e, bkB=bkB, dst=dst: e.tensor_copy(out=dst, in_=banks[bkB][:, :]),
                                             bres[bkB].rd() + bres[bkA].rd() + (dres.wr() if fresh else dres.rs))
                                bres[bkB].did_read(o_cp)
                                bres[bkA].did_read(o_cp)
                                dres.did_write(o_cp, fresh=fresh)
                                continue
                            o_t1 = P.add("dve", lambda e, bkA=bkA, ci=ci: e.tensor_tensor(
                                out=t1[0][:], in0=banks[bkA][:, :], in1=cs[ci][:], op=ALU.mult),
                                bres[bkA].rd() + cs_res[ci].rd() + t_res[0].wr() + [o_qb])
                            bres[bkA].did_read(o_t1)
                            cs_res[ci].did_read(o_t1)
                            o_t2 = P.add("dve", lambda e, bkB=bkB, ci=ci: e.tensor_tensor(
                                out=t2[0][:], in0=banks[bkB][:, :], in1=sn[ci][:], op=ALU.mult),
                                bres[bkB].rd() + cs_res[ci].rd() + t_res[0].wr())
                            bres[bkB].did_read(o_t2)
                            cs_res[ci].did_read(o_t2)
                            t_res[0].did_write(o_t1)
                            t_res[0].did_write(o_t2, fresh=False)
                            if pi5 < 4:
                                dst = sqT[:, pi5, t * TS:(t + 1) * TS]
                                dres = sq_res
                            else:
                                dst = skT[:, t * TS:(t + 1) * TS]
                                dres = sk_res
                            fresh = (t == 0 and pi5 in (0, 4))
                            if OPTS["rope"] == 2:
                                o_cp = P.add("dve", lambda e, dst=dst: e.tensor_copy(out=dst, in_=t1[0][:]),
                                             t_res[0].rd() + (dres.wr() if fresh else dres.rs))
                                t_res[0].did_read(o_cp)
                                dres.did_write(o_cp, fresh=fresh)
                                continue
                            o_ad = P.add("dve", lambda e, dst=dst: e.tensor_tensor(out=dst, in0=t1[0][:], in1=t2[0][:], op=ALU.add),
                                         t_res[0].rd() + (dres.wr() if fresh else dres.rs))
                            t_res[0].did_read(o_ad)
                            dres.did_write(o_ad, fresh=fresh)
                    for t in range(NT):
                        proj_v_tok(wsv, wsv_res, t, vts, vs_res, fresh=(t == 0))
                    if OPTS["swa_lvl"] < 2:
                        return
                    Pinfo = {}

                    def make_sqk(kb, pair):
                        sbk = next_pbank()
                        W = 256 if kb < 15 else 128

                        def fn(e, kb=kb, pair=pair, sbk=sbk, W=W):
                            outv = banks[sbk][:, 0:2 * W].rearrange("p (a w) -> p a w", a=2)
                            e.matmul(outv, lhsT=skT[:, kb * 128:(kb + 1) * 128],
                                     rhs=sqT[:, 2 * pair:2 * pair + 2, kb * 128:kb * 128 + W], start=True, stop=False)
                            return e.matmul(outv, lhsT=ident_bf[:],
                                            rhs=mswa_bf[:].rearrange("p (a w) -> p a w", a=2)[:, :, 0:W], start=False, stop=True)

                        o = P.add("pe", fn, sk_res.rd() + sq_res.rd() + bres[sbk].wr() + [ld_ms, ld_id])
                        sk_res.did_read(o)
                        sq_res.did_read(o)
                        bres[sbk].did_write(o)
                        return o, sbk, W

                    def make_sexp(kb, pair, qk):
                        o_qk, sbk, W = qk
                        pi = CT["ps_ctr"] % NPS
                        CT["ps_ctr"] += 1
                        o = P.add("act", lambda e, sbk=sbk, W=W, pi=pi: e.activation(
                            out=Ps[pi][:, 0:2 * W], in_=banks[sbk][:, 0:2 * W], func=AF.Exp, scale=SCALE),
                            bres[sbk].rd() + Ps_res[pi].wr())
                        bres[sbk].did_read(o)
                        Ps_res[pi].did_write(o)
                        Pinfo[(kb, pair)] = (pi, W)

                    def make_spv(qbk):
                        ob = 3 + 2 * (CT["qblk_ctr"] % 2)
                        db = ob + 1
                        ri = CT["qblk_ctr"] % 2
                        CT["qblk_ctr"] += 1

                        def fn(e, qbk=qbk, ob=ob, db=db):
                            ins = None
                            for (bank_i, is_den) in ((ob, False), (db, True)):
                                for pair in range(2):
                                    outv = banks[bank_i][:, pair * 256:(pair + 1) * 256].rearrange("p (a w) -> p a w", a=2)
                                    first = True
                                    if qbk > 0:
                                        pi, W = Pinfo[(qbk - 1, pair)]
                                        rhs = Ps[pi][:, 0:2 * W].rearrange("p (a w) -> p a w", a=2)[:, :, 128:256]
                                        lhsT = ones_bf[:] if is_den else vts[:, qbk - 1, :]
                                        ins = e.matmul(outv, lhsT=lhsT, rhs=rhs, start=True, stop=False)
                                        first = False
                                    pi, W = Pinfo[(qbk, pair)]
                                    rhs = Ps[pi][:, 0:2 * W].rearrange("p (a w) -> p a w", a=2)[:, :, 0:128]
                                    lhsT = ones_bf[:] if is_den else vts[:, qbk, :]
                                    ins = e.matmul(outv, lhsT=lhsT, rhs=rhs, start=first, stop=True)
                            return ins

                        deps = vs_res.rd() + bres[ob].wr() + bres[db].wr()
                        used = []
                        for pair in range(2):
                            for kk in ((qbk - 1, qbk) if qbk > 0 else (qbk,)):
                                pi, W = Pinfo[(kk, pair)]
                                deps = deps + Ps_res[pi].rd()
                                used.append(pi)
                        o = P.add("pe", fn, deps)
                        for pi in used:
                            Ps_res[pi].did_read(o)
                        vs_res.did_read(o)
                        bres[ob].did_write(o)
                        bres[db].did_write(o)
                        rd = rds[ri]
                        o_a = P.add("dve", lambda e, rd=rd, db=db: e.tensor_tensor(
                            out=rd[:], in0=banks[db][:, :], in1=esinkB[:, 4 * g:4 * g + 4, :].rearrange("p a w -> p (a w)"), op=ALU.add),
                            [o, o_esb] + rds_res[ri].wr())
                        bres[db].did_read(o_a)
                        o_r = P.add("dve", lambda e, rd=rd: e.reciprocal(out=rd[:], in_=rd[:]), [o_a])
                        rds_res[ri].did_write(o_r)
                        o_m = P.add("dve", lambda e, rd=rd, ob=ob, qbk=qbk: e.tensor_tensor(
                            out=mix_v[:, 8 + 4 * g:8 + 4 * g + 4, qbk * 128:(qbk + 1) * 128],
                            in0=banks[ob][:, :].rearrange("p (a w) -> p a w", a=4),
                            in1=rd[:].rearrange("p (a w) -> p a w", a=4), op=ALU.mult),
                            [o, o_r] + mix_res.rs, keep=True)
                        bres[ob].did_read(o_m)
                        rds_res[ri].did_read(o_m)
                        mix_res.did_write(o_m, fresh=False)

                    pend = [make_sqk(0, 0), make_sqk(0, 1)]
                    for kb in range(16):
                        nxt = []
                        if kb + 1 < 16:
                            nxt = [make_sqk(kb + 1, 0), make_sqk(kb + 1, 1)] if False else []
                        make_sexp(kb, 0, pend[0])
                        make_sexp(kb, 1, pend[1])
                        if kb + 1 < 16:
                            pend_next = [make_sqk(kb + 1, 0), make_sqk(kb + 1, 1)]
                        if OPTS["swa_lvl"] >= 3:
                            make_spv(kb)
                        if kb + 1 < 16:
                            pend = pend_next
                for g in range(OPTS["nswa"] if OPTS["swa"] else 0):
                    swa_group(g)
                if "d_mix" in dbg:
                    dump("d_mix", mixT[:], mix_res.rd())
                P.emit_block()
        if stop == "A":
            P.add("sp", None, final_deps)
            P.emit_block()
            return nc

        with ExitStack() as sc_2:
            ya = sb(sc_2, "ya", [128, KC, TS], F32)
            x1 = sb(sc_2, "x1", [128, KC, TS], F32)
            h2 = sb(sc_2, "h2", [128, KC, TS], BF16)
            abuf = [sb(sc_2, f"abuf{i}", [128, 8, TS], BF16) for i in range(2)]
            NSL = 3
            ring = [sb(sc_2, f"ring{i}", [128, 4096], BF16) for i in range(NSL)]
            ring_res = [Res() for _ in range(NSL)]
            NSQ = 4
            sqc = [sb(sc_2, f"sqc{i}", [128, TS], BF16) for i in range(NSQ)]
            sqc_res = [Res() for _ in range(NSQ)]
            rt = [sb(sc_2, f"rt{i}", [128, TS], F32) for i in range(2)]
            rt_res = [Res(), Res()]
            rtmp = sb(sc_2, "rtmp", [128, TS], F32)
            rtmp_res = Res()
            ya_c = [Res() for _ in range(KC)]
            x1_c = [Res() for _ in range(KC)]
            h2_res = Res()
            a_res = [Res(), Res()]
            CT["ring_ctr"] = 0
            CT["ob_ctr"] = 0
            CT["sq_ctr"] = 0
            CT["rt_ctr"] = 0
            SBK = [6, 7]
            stat_ctr = [0]
            pending = []

            def load_w(src_ap, view, cvk="A"):
                si = CT["ring_ctr"] % NSL
                CT["ring_ctr"] += 1
                if view == 16:
                    dst = ring[si][:].rearrange("p (k n) -> p k n", k=16)
                else:
                    dst = ring[si][:].rearrange("p (k n) -> p k n", k=8)
                ld = dma("sp", dst, src_ap, ring_res[si].wr() + cvdep[cvk], f"ring{si}")
                ring_res[si].did_write(ld)
                return si, dst

            def next_ob():
                b = CT["ob_ctr"] % 6
                CT["ob_ctr"] += 1
                return b

            def stats_add(src_ap, src_deps, sbk, first, last):
                qi = CT["sq_ctr"] % NSQ
                CT["sq_ctr"] += 1
                o_s = P.add("act", lambda e, qi=qi: e.activation(out=sqc[qi][:], in_=src_ap, func=AF.Square),
                            src_deps + sqc_res[qi].wr())
                sqc_res[qi].did_write(o_s)

                def mk(qi=qi, sbk=sbk, first=first, last=last):
                    o_m = P.add("pe", lambda e: e.matmul(banks[sbk][:, :], lhsT=ones_bf[:], rhs=sqc[qi][:], start=first, stop=last),
                                sqc_res[qi].rd() + (bres[sbk].wr() if first else []))
                    sqc_res[qi].did_read(o_m)
                    bres[sbk].did_write(o_m, fresh=first)
                    return o_m

                pending.append(mk)
                return o_s

            last_stat = [None]

            def flush_stats(keep):
                while len(pending) > keep:
                    last_stat[0] = pending.pop(0)()

            def make_rstd(sbk):
                flush_stats(0)
                o1 = P.add("act", lambda e: e.activation(out=rtmp[:], in_=banks[sbk][:, :], func=AF.Sqrt, bias=EPS, scale=1.0 / D),
                           [last_stat[0]] + bres[sbk].rd() + rtmp_res.wr())
                bres[sbk].did_read(o1)
                rtmp_res.did_write(o1)
                o2 = P.add("dve", lambda e: e.reciprocal(out=banks[sbk][:, :], in_=rtmp[:]), [o1] + bres[sbk].wr())
                rtmp_res.did_read(o2)
                bres[sbk].did_write(o2)
                return o2

            out_stores = []
            cvdep = {}

            def p2_tile(j):
                tsl = slice(j * TS, (j + 1) * TS)
                for q in range(4):
                    deps = []
                    for m in range(4 * q, 4 * q + 4):
                        deps += x1_c[m].wr()
                    ldx = dma("pool", x1[:, 4 * q:4 * q + 4, :], xT_v[:, 4 * q:4 * q + 4, tsl], deps, f"ldx{q}")
                    for m in range(4 * q, 4 * q + 4):
                        x1_c[m].did_write(ldx)
                if j == 0:
                    issue_conv(max(0, NCVA - len(conv_ops)))
                    cvdep["A"] = [conv_ops[min(NCVA, len(conv_ops)) - 1]]
                    issue_conv(len(conv_jobs))
                    cvdep["B"] = [conv_ops[-1]]
                sb1 = SBK[stat_ctr[0] % 2]
                stat_ctr[0] += 1
                for oc in range(8):
                    si, wv16 = load_w(wo_b[oc], 16)
                    for mm in range(2):
                        m = 2 * oc + mm
                        bk = next_ob()

                        def fn(e, wv16=wv16, mm=mm, bk=bk):
                            ins = None
                            for k in range(KC):
                                ins = e.matmul(banks[bk][:, :], lhsT=wv16[:, k, mm * 128:(mm + 1) * 128], rhs=mix_v[:, k, tsl],
                                               start=(k == 0), stop=(k == KC - 1))
                            return ins

                        o = P.add("pe", fn, ring_res[si].rd() + mix_res.rd() + bres[bk].wr())
                        ring_res[si].did_read(o)
                        mix_res.did_read(o)
                        bres[bk].did_write(o)
                        flush_stats(2)
                        ev = P.add("dve", lambda e, bk=bk, m=m: e.tensor_copy(out=ya[:, m, :], in_=banks[bk][:, :]),
                                   bres[bk].rd() + ya_c[m].wr())
                        bres[bk].did_read(ev)
                        ya_c[m].did_write(ev)
                        o_s = stats_add(ya[:, m, :], [ev], sb1, first=(m == 0), last=(m == KC - 1))
                        ya_c[m].did_read(o_s)
                o_rs = make_rstd(sb1)
                sb2 = SBK[stat_ctr[0] % 2]
                stat_ctr[0] += 1
                for m in range(KC):
                    o1 = P.add("dve", lambda e, m=m: e.scalar_tensor_tensor(out=ya[:, m, :], in0=ya[:, m, :], scalar=G1[:, m:m + 1],
                                                                            in1=banks[sb1][:, :], op0=ALU.mult, op1=ALU.mult),
                               ya_c[m].wr() + [o_rs, oG1])
                    bres[sb1].did_read(o1)
                    ya_c[m].did_write(o1)
                    o2 = P.add("dve", lambda e, m=m: e.tensor_tensor(out=x1[:, m, :], in0=x1[:, m, :], in1=ya[:, m, :], op=ALU.add),
                               [o1] + x1_c[m].wr())
                    ya_c[m].did_read(o2)
                    x1_c[m].did_write(o2)
                    o_s = stats_add(x1[:, m, :], [o2], sb2, first=(m == 0), last=(m == KC - 1))
                    x1_c[m].did_read(o_s)
                    flush_stats(0)
                o_rs2 = make_rstd(sb2)
                h2_ops = []
                for m in range(KC):
                    o3 = P.add("dve", lambda e, m=m: e.tensor_tensor(out=ya[:, m, :], in0=x1[:, m, :], in1=banks[sb2][:, :], op=ALU.mult),
                               ya_c[m].wr() + x1_c[m].rd() + [o_rs2])
                    bres[sb2].did_read(o3)
                    x1_c[m].did_read(o3)
                    ya_c[m].did_write(o3)
                    o4 = P.add("act", lambda e, m=m: e.activation(out=h2[:, m, :], in_=ya[:, m, :], func=AF.Identity,
                                                                  bias=modT[:, 48 + m:48 + m + 1], scale=A2[:, m:m + 1]),
                               [o3, oA2] + (h2_res.wr() if m == 0 else h2_res.rs))
                    ya_c[m].did_read(o4)
                    h2_res.did_write(o4, fresh=(m == 0))
                sb3 = SBK[stat_ctr[0] % 2]
                stat_ctr[0] += 1
                for g in range(8):
                    ai = g % 2
                    for ut in range(4):
                        si, wv16 = load_w(wu_b[g * 4 + ut], 16, "A" if g < 5 else "B")
                        for mm in range(2):
                            fc = 2 * ut + mm
                            bk = next_ob()

                            def fn(e, wv16=wv16, mm=mm, bk=bk):
                                ins = None
                                for k in range(KC):
                                    ins = e.matmul(banks[bk][:, :], lhsT=wv16[:, k, mm * 128:(mm + 1) * 128], rhs=h2[:, k, :],
                                                   start=(k == 0), stop=(k == KC - 1))
                                return ins

                            o = P.add("pe", fn, ring_res[si].rd() + h2_res.rd() + bres[bk].wr())
                            ring_res[si].did_read(o)
                            h2_res.did_read(o)
                            bres[bk].did_write(o)
                            flush_stats(2)
                            ri = CT["rt_ctr"] % 2
                            CT["rt_ctr"] += 1
                            o_r = P.add("act", lambda e, ri=ri, bk=bk: e.activation(out=rt[ri][:], in_=banks[bk][:, :], func=AF.Relu),
                                        bres[bk].rd() + rt_res[ri].wr())
                            bres[bk].did_read(o_r)
                            rt_res[ri].did_write(o_r)
                            o_a = P.add("dve", lambda e, ri=ri, bk=bk, ai=ai, fc=fc: e.scalar_tensor_tensor(
                                out=abuf[ai][:, fc, :], in0=banks[bk][:, :], scalar=0.0, in1=rt[ri][:], op0=ALU.max, op1=ALU.mult),
                                bres[bk].rd() + rt_res[ri].rd() + (a_res[ai].wr() if fc == 0 else a_res[ai].rs))
                            bres[bk].did_read(o_a)
                            rt_res[ri].did_read(o_a)
                            a_res[ai].did_write(o_a, fresh=(fc == 0))
                    for dt in range(4):
                        si, wv8 = load_w(wd_b[g * 4 + dt], 8, "A" if g < 5 else "B")
                        for mm in range(4):
                            m = 4 * dt + mm
                            bk = next_ob()

                            def fn(e, wv8=wv8, mm=mm, bk=bk, ai=ai):
                                ins = None
                                for fc in range(8):
                                    ins = e.matmul(banks[bk][:, :], lhsT=wv8[:, fc, mm * 128:(mm + 1) * 128], rhs=abuf[ai][:, fc, :],
                                                   start=(fc == 0), stop=(fc == 7))
                                return ins

                            o = P.add("pe", fn, ring_res[si].rd() + a_res[ai].rd() + bres[bk].wr())
                            ring_res[si].did_read(o)
                            a_res[ai].did_read(o)
                            bres[bk].did_write(o)
                            flush_stats(2)
                            if g == 0:
                                ev = P.add("dve", lambda e, bk=bk, m=m: e.tensor_copy(out=ya[:, m, :], in_=banks[bk][:, :]),
                                           bres[bk].rd() + ya_c[m].wr())
                            else:
                                ev = P.add("dve", lambda e, bk=bk, m=m: e.tensor_tensor(out=ya[:, m, :], in0=banks[bk][:, :], in1=ya[:, m, :],
                                                                                         op=ALU.add),
                                           bres[bk].rd() + ya_c[m].wr())
                            ya_c[m].did_write(ev)
                            bres[bk].did_read(ev)
                            if g == 7:
                                o_s = stats_add(ya[:, m, :], [ev], sb3, first=(m == 0), last=(m == KC - 1))
                                ya_c[m].did_read(o_s)
                o_rs3 = make_rstd(sb3)
                for q in range(4):
                    fin = []
                    for m in range(4 * q, 4 * q + 4):
                        o5 = P.add("dve", lambda e, m=m: e.scalar_tensor_tensor(out=ya[:, m, :], in0=ya[:, m, :], scalar=G2[:, m:m + 1],
                                                                                in1=banks[sb3][:, :], op0=ALU.mult, op1=ALU.mult),
                                   ya_c[m].wr() + [o_rs3, oG2])
                        bres[sb3].did_read(o5)
                        ya_c[m].did_write(o5)
                        o6 = P.add("dve", lambda e, m=m: e.tensor_tensor(out=x1[:, m, :], in0=x1[:, m, :], in1=ya[:, m, :], op=ALU.add),
                                   [o5] + x1_c[m].wr())
                        ya_c[m].did_read(o6)
                        x1_c[m].did_write(o6)
                        fin.append(o6)
                    st = dma("pool", outT_v[:, 4 * q:4 * q + 4, tsl], x1[:, 4 * q:4 * q + 4, :], fin, f"st{q}")
                    for m in range(4 * q, 4 * q + 4):
                        x1_c[m].did_read(st)
                    out_stores.append(st)

            for j in range(NT):
                p2_tile(j)
            P.add("sp", None, out_stores + final_deps)
            P.emit_block()
    return nc


def _host_inputs(inputs):
    f32 = np.float32
    x = np.asarray(inputs["x"], f32)
    c = np.asarray(inputs["c"], f32)
    consts = _consts()
    shared = {
        "w_mod": np.ascontiguousarray(np.asarray(inputs["w_mod"], f32)[0]),
        "b_modT": np.ascontiguousarray(np.asarray(inputs["b_mod"], f32)[0].reshape(96, 128).T),
        "gvec": np.ascontiguousarray(np.stack([np.asarray(inputs[k], f32)[0].reshape(KC, 128).T for k in
                                               ("g_pre_mix", "g_post_mix", "g_pre_mlp", "g_post_mlp")], axis=1)),
        "w_in": np.ascontiguousarray(np.asarray(inputs["w_in"], f32)[0]),
        "b_fg": np.ascontiguousarray(np.asarray(inputs["b_forget"], f32)[0].reshape(8, 1)),
        "sinksB": np.ascontiguousarray(np.broadcast_to(np.asarray(inputs["swa_sinks"], f32)[0][None, :], (128, 8))),
        "w_out": np.ascontiguousarray(np.asarray(inputs["w_out"], f32)[0]),
        "w_up": np.ascontiguousarray(np.asarray(inputs["w_up"], f32)[0]),
        "w_down": np.ascontiguousarray(np.asarray(inputs["w_down"], f32)[0]),
    }
    shared.update(consts)
    in_maps = []
    for b in range(8):
        m = dict(shared)
        m["xT"] = np.ascontiguousarray(x[b].T)
        m["c_col"] = np.ascontiguousarray(c[b].reshape(KC, 128).T)
        in_maps.append(m)
    return in_maps


def kernel(**inputs):
    in_maps = _host_inputs(inputs)
    nc = build("full")
    res = run_bass_kernel_spmd(nc, in_maps, core_ids=list(range(8)))
    out = np.stack([np.ascontiguousarray(r["outT"].T) for r in res.results], axis=0)
    return out.astype(np.float32)
```

```python
import numpy as np
import ml_dtypes
import concourse.bass as bass
import concourse.mybir as mybir
from concourse.bass_utils import run_bass_kernel_spmd

F32 = mybir.dt.float32
BF16 = mybir.dt.bfloat16
AF = mybir.ActivationFunctionType
ALU = mybir.AluOpType

D = 2048
S = 2048
KC = 16
NT = 4
TS = 512
DFF = 8192
NMOD = 6 * D
INW = 4616
EPS = 1e-6
SCALE = 128 ** -0.5
C_FQ, C_FK, C_FV, C_FG, C_SQ, C_SK, C_SV = 0, 1024, 2048, 3072, 3080, 4104, 4360
MASKV = -30000.0

ENG_ATTR = {"pe": "tensor", "act": "scalar", "dve": "vector", "pool": "gpsimd", "sp": "sync"}


class Op:
    __slots__ = ("eng", "fn", "deps", "dsem", "sig", "ncons", "blk", "keep")


class Res:
    def __init__(self):
        self.ws = []
        self.rs = []

    def rd(self):
        return list(self.ws)

    def wr(self):
        return list(self.ws) + list(self.rs)

    def did_read(self, op):
        self.rs.append(op)

    def did_write(self, op, fresh=True):
        if fresh:
            self.ws = [op]
            self.rs = []
        else:
            self.ws.append(op)


class Prog:
    def __init__(self, nc, stack):
        self.nc = nc
        self.stack = stack
        self.ops = []
        self.sem = {}
        self.cnt = {}
        self.waited = {e: {} for e in ENG_ATTR}
        self.blk = 0
        for e in ("pe", "act", "dve", "pool"):
            self._sem("c_" + e)

    def _sem(self, name):
        if name not in self.sem:
            self.sem[name] = self.stack.enter_context(self.nc.semaphore(name))
            self.cnt[name] = 0
        return self.sem[name]

    def add(self, eng, fn, deps=(), dsem=None, keep=False):
        op = Op()
        op.eng = eng
        op.fn = fn
        op.deps = [d for d in deps if d is not None]
        op.dsem = dsem
        op.sig = None
        op.ncons = 0
        op.blk = self.blk
        op.keep = keep
        for d in op.deps:
            d.ncons += 1
        if dsem is not None:
            self._sem(dsem)
        self.ops.append(op)
        return op

    def emit_block(self):
        nc = self.nc
        ops = self.ops
        self.ops = []
        for op in ops:
            if op.dsem is not None:
                self.cnt[op.dsem] += 16
                op.sig = (op.dsem, self.cnt[op.dsem])
            elif op.fn is not None and (op.ncons > 0 or op.keep):
                nm = "c_" + op.eng
                self.cnt[nm] += 1
                op.sig = (nm, self.cnt[nm])
        cur = self.blk
        with nc.Block() as block:
            for eng, attr in ENG_ATTR.items():
                mine = [op for op in ops if op.eng == eng]
                if not mine:
                    continue

                def body(e, mine=mine, eng=eng):
                    waited = self.waited[eng]
                    for op in mine:
                        for d in op.deps:
                            if d.blk != cur and d.dsem is None:
                                continue
                            sname, val = d.sig
                            if waited.get(sname, 0) < val:
                                e.wait_ge(self.sem[sname], val)
                                waited[sname] = val
                        if op.fn is not None:
                            ins = op.fn(e)
                            if op.sig is not None:
                                ins.then_inc(self.sem[op.sig[0]], 16 if op.dsem is not None else 1)

                getattr(block, attr)(body)
        self.blk += 1


def _consts():
    bf = ml_dtypes.bfloat16
    ident = np.eye(128, dtype=np.float32)
    perm = np.zeros((128, 128), np.float32)
    for m in range(128):
        perm[(m + 64) % 128, m] = 1.0
    kk = np.arange(128)[:, None]
    qq = np.arange(128)[None, :]
    maskc = np.where(kk <= qq, 0.0, MASKV).astype(np.float32)
    mask2 = np.where(kk > qq, 0.0, MASKV).astype(np.float32)
    maskswa = np.concatenate([maskc, mask2, maskc, mask2], axis=1)
    sel = np.zeros((128, 8, 128), np.float32)
    for h in range(8):
        for r in (h, 32 + h, 64 + h):
            sel[r, h, :] = 1.0
    half = 64
    inv = (1.0 / (10000.0 ** (np.arange(half, dtype=np.float32) * (2.0 / 128)))).astype(np.float32)
    ang = (np.arange(S, dtype=np.float32)[None, :] * inv[:, None]).astype(np.float32)
    cos = np.cos(ang).astype(np.float32)
    sin = np.sin(ang).astype(np.float32)
    cosT = np.concatenate([cos, cos], axis=0)
    sinT = np.concatenate([-sin, sin], axis=0)
    return {
        "k_ident": ident.astype(bf), "k_perm": perm.astype(bf), "k_maskc": maskc.astype(bf),
        "k_maskswa": maskswa.astype(bf), "k_sel": sel.astype(bf), "k_cos": cosT, "k_sin": sinT,
        "k_identf": np.eye(8, dtype=np.float32),
    }


OPTS = {"nfox": 8, "mod_bg": True, "swa": True, "nswa": 2, "swa_lvl": 9, "rope": 9}


def build(stop="full", debug=()):
    nc = bass.Bass("TRN2", target_bir_lowering=False)
    from contextlib import ExitStack

    def din(name, shape, dt=F32):
        return nc.dram_tensor(name, list(shape), dt, kind="ExternalInput").ap()

    xT = din("xT", [D, S])
    c_col = din("c_col", [128, KC])
    w_mod = din("w_mod", [D, NMOD])
    b_modT = din("b_modT", [128, 96])
    gvec = din("gvec", [128, 4, KC])
    w_in = din("w_in", [D, INW])
    b_fg = din("b_fg", [8, 1])
    sinksB = din("sinksB", [128, 8])
    w_out = din("w_out", [D, D])
    w_up = din("w_up", [D, DFF])
    w_down = din("w_down", [DFF, D])
    k_ident = din("k_ident", [128, 128], BF16)
    k_perm = din("k_perm", [128, 128], BF16)
    k_maskc = din("k_maskc", [128, 128], BF16)
    k_maskswa = din("k_maskswa", [128, 512], BF16)
    k_sel = din("k_sel", [128, 8, 128], BF16)
    k_cos = din("k_cos", [128, S])
    k_sin = din("k_sin", [128, S])
    k_identf = din("k_identf", [8, 8])
    outT = nc.dram_tensor("outT", [D, S], F32, kind="ExternalOutput").ap()
    dbg = {}
    for nm, shape, dt in debug:
        dbg[nm] = nc.dram_tensor(nm, list(shape), dt, kind="ExternalOutput").ap()

    xT_v = xT.rearrange("(k p) s -> p k s", p=128)
    outT_v = outT.rearrange("(k p) s -> p k s", p=128)
    w_mod_v = w_mod.rearrange("(k p) n -> p k n", p=128)
    w_in_v = w_in.rearrange("(k p) n -> p k n", p=128)
    w_out_v = w_out.rearrange("(k p) n -> p k n", p=128)
    w_up_v = w_up.rearrange("(k p) n -> p k n", p=128)
    w_down_v = w_down.rearrange("(k p) n -> p k n", p=128)
    wo_b = nc.dram_tensor("wo_b", [8, 128, 16, 256], BF16).ap()
    wu_b = nc.dram_tensor("wu_b", [32, 128, 16, 256], BF16).ap()
    wd_b = nc.dram_tensor("wd_b", [32, 128, 8, 512], BF16).ap()
    conv_jobs = []
    for oc in range(8):
        conv_jobs.append((wo_b[oc], w_out_v[:, :, oc * 256:(oc + 1) * 256]))
    for g in range(8):
        for ut in range(4):
            c0 = g * 1024 + ut * 256
            conv_jobs.append((wu_b[g * 4 + ut], w_up_v[:, :, c0:c0 + 256]))
        for dt in range(4):
            conv_jobs.append((wd_b[g * 4 + dt], w_down_v[:, g * 8:(g + 1) * 8, dt * 512:(dt + 1) * 512]))
    conv_ops = []

    with ExitStack() as top:
        P = Prog(nc, top)

        def sb(stack, name, shape, dt):
            return stack.enter_context(nc.sbuf_tensor(name, list(shape), dt))

        banks = [top.enter_context(nc.psum_tensor(f"bank{i}", [128, 512], F32)) for i in range(8)]
        bres = [Res() for _ in range(8)]
        mixT = sb(top, "mixT", [128, KC * S], BF16)
        mix_v = mixT[:].rearrange("p (k s) -> p k s", k=KC)
        mix_res = Res()
        ones_bf = sb(top, "ones_bf", [128, 128], BF16)
        ident_bf = sb(top, "ident_bf", [128, 128], BF16)
        modT = sb(top, "modT", [128, 96], F32)
        bmod_sb = sb(top, "bmod_sb", [128, 96], F32)
        gv = sb(top, "gv", [128, 4, KC], F32)
        A1 = sb(top, "A1", [128, KC], F32)
        G1 = sb(top, "G1", [128, KC], F32)
        A2 = sb(top, "A2", [128, KC], F32)
        G2 = sb(top, "G2", [128, KC], F32)
        cc = sb(top, "cc", [128, KC], F32)
        cond_bf = sb(top, "cond_bf", [128, KC], BF16)

        final_deps = []
        CT = {k: 0 for k in ['pb_ctr', 'sbank_ctr', 'jt_ctr', 'ps_ctr', 'rp_ctr', 'qblk_ctr', 'wrot_ctr', 'cs_ctr', 'ring_ctr', 'ob_ctr', 'sq_ctr', 'rt_ctr', 'mod_i']}

        def dma(eng, out, in_, deps, dsem):
            return P.add(eng, lambda e: e.dma_start(out=out, in_=in_), deps, dsem=dsem)

        NCVA = 40

        def issue_conv(n):
            for _ in range(n):
                if conv_jobs:
                    dst, srcap = conv_jobs.pop(0)
                    conv_ops.append(dma("pool", dst, srcap, [], "cvA" if len(conv_ops) < NCVA else "cvB"))

        def dump(name, src_ap, deps):
            if name in dbg:
                op = dma("sp", dbg[name], src_ap, deps, dsem="dbg_" + name)
                final_deps.append(op)

        ld_c = dma("sp", cc[:], c_col, [], "ld_small0")
        ld_bm = dma("sp", bmod_sb[:], b_modT, [], "ld_small1")
        ld_gv = dma("sp", gv[:], gvec, [], "ld_small2")
        ld_id = dma("sp", ident_bf[:], k_ident, [], "ld_small3")
        op_ones = P.add("dve", lambda e: e.memset(ones_bf[:], 1.0), [], keep=True)
        op_cond = P.add("act", lambda e: e.activation(out=cond_bf[:], in_=cc[:], func=AF.Silu), [ld_c], keep=True)

        MB = 7

        def mod_dma(wm, wm_res, slot, c0, ncols):
            w = wm[slot]
            r = wm_res[slot]
            ld = dma("pool", w[:, :, 0:ncols], w_mod_v[:, :, c0:c0 + ncols], r.wr(), f"wm{slot}")
            r.did_write(ld)
            return ld

        def mod_pe(wm, wm_res, slot, c0, ncols):
            w = wm[slot]
            r = wm_res[slot]

            def fn(e):
                ins = None
                for jj in range(ncols // 128):
                    j = c0 // 128 + jj
                    for k in range(KC):
                        ins = e.matmul(banks[MB][:, j:j + 1], lhsT=w[:, k, jj * 128:(jj + 1) * 128],
                                       rhs=cond_bf[:, k:k + 1], start=(k == 0), stop=(k == KC - 1))
                return ins

            mm = P.add("pe", fn, r.rd() + [op_cond] + bres[MB].wr())
            r.did_read(mm)
            bres[MB].did_write(mm, fresh=False)
            return mm

        def mod_tile(wm, wm_res, slot, c0, ncols):
            mod_dma(wm, wm_res, slot, c0, ncols)
            return mod_pe(wm, wm_res, slot, c0, ncols)

        def mod_finish(j0, j1, deps):
            op = P.add("dve", lambda e: e.tensor_tensor(out=modT[:, j0:j1], in0=banks[MB][:, j0:j1],
                                                        in1=bmod_sb[:, j0:j1], op=ALU.add),
                       deps + [ld_bm], keep=True)
            bres[MB].did_read(op)
            return op

        with ExitStack() as sc_h:
            hT = sb(sc_h, "hT", [128, KC, S], BF16)
            hT_res = Res()
            with ExitStack() as sc_n:
                wm = [sb(sc_n, f"wmN{i}", [128, KC, 512], BF16) for i in range(2)]
                wm_res = [Res(), Res()]
                sq = sb(sc_n, "sqN", [128, KC, TS], BF16)
                sq_res = Res()
                rstd = [sb(sc_n, f"rstdN{i}", [128, TS], F32) for i in range(2)]
                rstd_res = [Res(), Res()]
                xs_all = mixT[:].bitcast(F32).rearrange("p (i k s) -> p i k s", i=2, k=KC)
                xs_res = [Res(), Res()]

                def norm_stage1(t):
                    i = t % 2
                    xs = xs_all[:, i]
                    ld = dma("sp", xs, xT_v[:, :, t * TS:(t + 1) * TS], xs_res[i].wr(), f"xs{i}")
                    xs_res[i].did_write(ld)
                    o_sq = P.add("act", lambda e, xs=xs: e.activation(out=sq[:], in_=xs, func=AF.Square),
                                 xs_res[i].rd() + sq_res.wr())
                    xs_res[i].did_read(o_sq)
                    sq_res.did_write(o_sq)
                    bk = t % 2

                    def fn_ss(e, bk=bk):
                        ins = None
                        for k in range(KC):
                            ins = e.matmul(banks[bk][:, :], lhsT=ones_bf[:], rhs=sq[:, k, :],
                                           start=(k == 0), stop=(k == KC - 1))
                        return ins

                    o_ss = P.add("pe", fn_ss, sq_res.rd() + bres[bk].wr() + [op_ones])
                    sq_res.did_read(o_ss)
                    bres[bk].did_write(o_ss)
                    rs = rstd[i]
                    o_sqrt = P.add("act", lambda e, rs=rs, bk=bk: e.activation(out=rs[:], in_=banks[bk][:, :], func=AF.Sqrt,
                                                                                 bias=EPS, scale=1.0 / D),
                                   bres[bk].rd() + rstd_res[i].wr())
                    bres[bk].did_read(o_sqrt)
                    rstd_res[i].did_write(o_sqrt)
                    o_rec = P.add("dve", lambda e, rs=rs: e.reciprocal(out=rs[:], in_=rs[:]), rstd_res[i].rd())
                    rstd_res[i].did_write(o_rec)
                    o_mul = P.add("dve", lambda e, xs=xs, rs=rs: e.tensor_tensor(
                        out=xs, in0=xs, in1=rs[:].unsqueeze(1).broadcast_to([128, KC, TS]), op=ALU.mult),
                        rstd_res[i].rd() + xs_res[i].wr())
                    rstd_res[i].did_read(o_mul)
                    xs_res[i].did_write(o_mul)

                def norm_stage2(t, opA1):
                    i = t % 2
                    xs = xs_all[:, i]

                    def fn_h(e, xs=xs, t=t):
                        ins = None
                        for k in range(KC):
                            ins = e.activation(out=hT[:, k, t * TS:(t + 1) * TS], in_=xs[:, k, :], func=AF.Identity,
                                               bias=modT[:, k:k + 1], scale=A1[:, k:k + 1])
                        return ins

                    o_h = P.add("act", fn_h, xs_res[i].rd() + [opA1] + hT_res.wr(), keep=True)
                    xs_res[i].did_read(o_h)
                    hT_res.did_write(o_h, fresh=(t == 0))

                norm_stage1(0)
                norm_stage1(1)
                mms = [mod_tile(wm, wm_res, jt % 2, jt * 512, 512) for jt in range(8)]
                fin1 = mod_finish(0, 32, [mms[-1]])
                opA1 = P.add("dve", lambda e: e.scalar_tensor_tensor(out=A1[:], in0=modT[:, 16:32], scalar=1.0,
                                                                    in1=gv[:, 0, :], op0=ALU.add, op1=ALU.mult),
                             [fin1, ld_gv], keep=True)
                norm_stage2(0, opA1)
                norm_stage2(1, opA1)
                norm_stage1(2)
                norm_stage1(3)
                norm_stage2(2, opA1)
                norm_stage2(3, opA1)
                if "d_hT" in dbg:
                    dump("d_hT", hT[:], hT_res.rd())
                    dump("d_modT", modT[:], [fin1])
                P.emit_block()
            if stop == "N":
                fin = P.add("sp", None, final_deps)
                P.emit_block()
                return nc

            CQ = sb(sc_h, "CQ", [128, S], BF16)
            Ctok = sb(sc_h, "Ctok", [128, 128], F32)
            sel_bf = sb(sc_h, "sel_bf", [128, 8, 128], BF16)
            maskc_bf = sb(sc_h, "maskc_bf", [128, 128], BF16)
            ld_sel = dma("sp", sel_bf[:], k_sel, [], "ld_small0")
            ld_mc = dma("sp", maskc_bf[:], k_maskc, [], "ld_small1")

            with ExitStack() as sc_f:
                wg3 = sb(sc_f, "wg3", [128, KC, 72], BF16)
                b72 = sb(sc_f, "b72", [72, 1], F32)
                nb72 = sb(sc_f, "nb72", [72, 1], F32)
                Cf = sb(sc_f, "Cf", [72, S], F32)
                lf = sb(sc_f, "lf", [72, S], F32)
                onesF = sb(sc_f, "onesF", [72, S], F32)
                Thi = sb(sc_f, "Thi", [72, S], BF16)
                identF = sb(sc_f, "identF", [8, 8], F32)
                ld_if = dma("sp", identF[:], k_identf, [], "ld_small2")
                ms_w = P.add("dve", lambda e: e.memset(wg3[:], 0.0), [])
                ms_b = P.add("dve", lambda e: e.memset(b72[:], 0.0), [])
                ms_1 = P.add("dve", lambda e: e.memset(onesF[:], 1.0), [])
                ms_q = P.add("dve", lambda e: e.memset(CQ[:], 0.0), [])
                ldw = []
                ldb = []
                for i, r0 in enumerate((0, 32, 64)):
                    ldw.append(dma("pool", wg3[:, :, r0:r0 + 8], w_in_v[:, :, C_FG:C_FG + 8], [ms_w], f"wg{i}"))
                    ldb.append(dma("sp", b72[r0:r0 + 8, :], b_fg, [ms_b], f"bg{i}"))
                o_nb = P.add("dve", lambda e: e.tensor_scalar(out=nb72[:], in0=b72[:], scalar1=-1.0, scalar2=None,
                                                              op0=ALU.mult), ldb)
                lf_w = []
                for t in range(NT):
                    bk = t % 2

                    def fn_fg(e, t=t, bk=bk):
                        ins = None
                        for k in range(KC):
                            ins = e.matmul(banks[bk][0:72, :], lhsT=wg3[:, k, :], rhs=hT[:, k, t * TS:(t + 1) * TS],
                                           start=(k == 0), stop=(k == KC - 1))
                        return ins

                    o_fg = P.add("pe", fn_fg, ldw + hT_res.rd() + bres[bk].wr())
                    bres[bk].did_write(o_fg)
                    o_e = P.add("act", lambda e, t=t, bk=bk: e.activation(
                        out=lf[:, t * TS:(t + 1) * TS], in_=banks[bk][0:72, :], func=AF.Exp, bias=nb72[:, 0:1], scale=-1.0),
                        bres[bk].rd() + [o_nb])
                    bres[bk].did_read(o_e)
                    o_l = P.add("act", lambda e, t=t: e.activation(
                        out=lf[:, t * TS:(t + 1) * TS], in_=lf[:, t * TS:(t + 1) * TS], func=AF.Ln, bias=1.0, scale=1.0),
                        [o_e])
                    lf_w.append(o_l)
                o_scan = P.add("dve", lambda e: e.tensor_tensor_scan(out=Cf[:], data0=onesF[:], data1=lf[:], initial=0.0,
                                                                     op0=ALU.mult, op1=ALU.add), lf_w + [ms_1])
                o_neg = P.add("dve", lambda e: e.tensor_scalar(out=lf[:], in0=Cf[:], scalar1=-1.0, scalar2=None,
                                                               op0=ALU.mult), [o_scan])
                o_hi = P.add("dve", lambda e: e.tensor_copy(out=Thi[:], in_=lf[:]), [o_neg])
                o_r1 = P.add("dve", lambda e: e.tensor_tensor(out=onesF[:], in0=lf[:], in1=Thi[:], op=ALU.subtract), [o_hi])
                o_mid = P.add("dve", lambda e: e.tensor_copy(out=CQ[0:72, :], in_=onesF[:]), [o_r1, ms_q])
                o_r2 = P.add("dve", lambda e: e.tensor_tensor(out=lf[:], in0=onesF[:], in1=CQ[0:72, :], op=ALU.subtract), [o_mid])
                o_lo = P.add("dve", lambda e: e.tensor_copy(out=CQ[64:72, :], in_=lf[64:72, :]), [o_r2])
                o_hi2 = P.add("dve", lambda e: e.tensor_copy(out=CQ[0:32, :], in_=Thi[0:32, :]), [o_r2], keep=True)
                cq_ready = [o_lo, o_hi2]
                o_lo.keep = True

                def fn_tr(e):
                    ins = None
                    for blk in range(16):
                        ins = e.transpose(out=banks[2][:, blk * 8:(blk + 1) * 8], in_=Cf[0:8, blk * 128:(blk + 1) * 128],
                                          identity=identF[:])
                    return ins

                o_tr = P.add("pe", fn_tr, [o_scan, ld_if] + bres[2].wr())
                bres[2].did_write(o_tr)
                o_ct = P.add("dve", lambda e: e.tensor_copy(out=Ctok[:], in_=banks[2][:, 0:128]), [o_tr], keep=True)
                bres[2].did_read(o_ct)
                if "d_C" in dbg:
                    dump("d_C", Cf[:], [o_scan])
                    dump("d_Ctok", Ctok[:], [o_ct])
                    dump("d_CQ", CQ[:], cq_ready)
                P.emit_block()
            if stop == "FG":
                P.add("sp", None, final_deps)
                P.emit_block()
                return nc

            pbank_ctr = [0]

            def next_pbank():
                b = pbank_ctr[0] % 3
                pbank_ctr[0] += 1
                return b

            def proj_fm(wt, wres, t, evac_fn, evac_eng, dst_res, fresh):
                bk = next_pbank()

                def fn(e, bk=bk):
                    ins = None
                    for k in range(KC):
                        ins = e.matmul(banks[bk][:, :], lhsT=wt[:, k, :], rhs=hT[:, k, t * TS:(t + 1) * TS],
                                       start=(k == 0), stop=(k == KC - 1))
                    return ins

                o = P.add("pe", fn, wres.rd() + hT_res.rd() + bres[bk].wr())
                wres.did_read(o)
                bres[bk].did_write(o)
                ev = P.add(evac_eng, lambda e, bk=bk: evac_fn(e, banks[bk]), bres[bk].rd() + dst_res.wr() if fresh else bres[bk].rd() + dst_res.rs)
                bres[bk].did_read(ev)
                dst_res.did_write(ev, fresh=fresh)
                return o, ev, bk

            def proj_v_tok(wt, wres, blk4, vt, vres, fresh):
                bk = next_pbank()

                def fn(e, bk=bk):
                    ins = None
                    for bi in range(4):
                        blk = blk4 * 4 + bi
                        for k in range(KC):
                            ins = e.matmul(banks[bk][:, bi * 128:(bi + 1) * 128], lhsT=hT[:, k, blk * 128:(blk + 1) * 128],
                                           rhs=wt[:, k, :], start=(k == 0), stop=(k == KC - 1))
                    return ins

                o = P.add("pe", fn, wres.rd() + hT_res.rd() + bres[bk].wr())
                wres.did_read(o)
                bres[bk].did_write(o)
                ev = P.add("dve", lambda e, bk=bk: e.tensor_copy(
                    out=vt[:, blk4 * 4:(blk4 + 1) * 4, :], in_=banks[bk][:, :].rearrange("p (a d) -> p a d", a=4)),
                    bres[bk].rd() + (vres.wr() if fresh else vres.rs))
                bres[bk].did_read(ev)
                vres.did_write(ev, fresh=fresh)
                return ev

            with ExitStack() as sc_x:
                wm2 = [sb(sc_x, f"wmX{i}", [128, KC, 128], BF16) for i in range(2)]
                wm2_res = [Res(), Res()]
                wq = [sb(sc_x, f"wq{i}", [128, KC, 128], BF16) for i in range(2)]
                wk = [sb(sc_x, f"wk{i}", [128, KC, 128], BF16) for i in range(2)]
                wv = [sb(sc_x, f"wv{i}", [128, KC, 128], BF16) for i in range(2)]
                wq_res = [Res(), Res()]
                wk_res = [Res(), Res()]
                wv_res = [Res(), Res()]
                qT = [sb(sc_x, f"qT{i}", [128, S], BF16) for i in range(2)]
                kT = [sb(sc_x, f"kT{i}", [128, S], BF16) for i in range(2)]
                vtok = [sb(sc_x, f"vtok{i}", [128, 16, 128], BF16) for i in range(2)]
                q_res = [Res(), Res()]
                k_res = [Res(), Res()]
                v_res = [Res(), Res()]
                NPB = 4
                Pb = [sb(sc_x, f"Pb{i}", [128, TS], BF16) for i in range(NPB)]
                Pb_res = [Res() for _ in range(NPB)]
                rden = [sb(sc_x, f"rden{i}", [128, TS], F32) for i in range(2)]
                rden_res = [Res(), Res()]
                CT["pb_ctr"] = 0
                CT["sbank_ctr"] = 0
                mod_jobs = [(32 * 128 + i * 128) for i in range(64)]
                mod_ops = []
                CT["mod_i"] = 0
                CT["jt_ctr"] = 0
                def load_head(hh):
                    ss = hh % 2
                    for (wt, wres, c0, nm) in ((wq, wq_res, C_FQ, "wq"), (wk, wk_res, C_FK, "wk"), (wv, wv_res, C_FV, "wv")):
                        ld = dma("pool", wt[ss][:], w_in_v[:, :, c0 + hh * 128:c0 + (hh + 1) * 128], wres[ss].wr(), f"{nm}{ss}")
                        wres[ss].did_write(ld)

                def mod_step():
                    i = CT["mod_i"]
                    if not OPTS["mod_bg"] or i >= len(mod_jobs):
                        return
                    mod_ops.append(mod_pe(wm2, wm2_res, i % 2, mod_jobs[i], 128))
                    if i + 2 < len(mod_jobs):
                        mod_dma(wm2, wm2_res, i % 2, mod_jobs[i + 2], 128)
                    CT["mod_i"] += 1

                if OPTS["nfox"] > 0:
                    load_head(0)
                if OPTS["mod_bg"]:
                    mod_dma(wm2, wm2_res, 0, mod_jobs[0], 128)
                    mod_dma(wm2, wm2_res, 1, mod_jobs[1], 128)

                def fox_head(h):
                    s = h % 2
                    for t in range(NT):
                        proj_fm(wq[s], wq_res[s], t,
                                lambda e, bank, t=t, s=s: e.activation(out=qT[s][:, t * TS:(t + 1) * TS], in_=bank[:, :],
                                                                        func=AF.Copy, scale=SCALE),
                                "act", q_res[s], fresh=(t == 0))
                        mod_step()
                    for t in range(NT):
                        proj_fm(wk[s], wk_res[s], t,
                                lambda e, bank, t=t, s=s: e.tensor_copy(out=kT[s][:, t * TS:(t + 1) * TS], in_=bank[:, :]),
                                "dve", k_res[s], fresh=(t == 0))
                        mod_step()
                    if h + 1 < OPTS["nfox"]:
                        load_head(h + 1)
                    for b4 in range(4):
                        proj_v_tok(wv[s], wv_res[s], b4, vtok[s], v_res[s], fresh=(b4 == 0))
                    issue_conv(5)
                    def fox_qtile(j):
                        ob = 3 + 2 * (CT["jt_ctr"] % 2)
                        db = ob + 1
                        rd_i = CT["jt_ctr"] % 2
                        CT["jt_ctr"] += 1
                        nkb = 4 * j + 4
                        qk_ops = {}
                        ex_ops = {}

                        def make_qk(kb):
                            sbk = CT["sbank_ctr"] % 3
                            CT["sbank_ctr"] += 1
                            c0 = max(0, kb - 4 * j) * 128

                            def fn(e, kb=kb, sbk=sbk, c0=c0):
                                e.matmul(banks[sbk][:, c0:TS], lhsT=kT[s][:, kb * 128:(kb + 1) * 128],
                                         rhs=qT[s][:, j * TS + c0:(j + 1) * TS], start=True, stop=False)
                                diag = kb >= 4 * j
                                ins = e.matmul(banks[sbk][:, c0:TS], lhsT=sel_bf[0:72, h, :],
                                               rhs=CQ[0:72, j * TS + c0:(j + 1) * TS], start=False, stop=not diag)
                                if diag:
                                    ins = e.matmul(banks[sbk][:, c0:c0 + 128], lhsT=ident_bf[:], rhs=maskc_bf[:],
                                                   start=False, stop=True)
                                return ins

                            o = P.add("pe", fn, k_res[s].rd() + q_res[s].rd() + cq_ready + [ld_sel, ld_mc, ld_id] + bres[sbk].wr())
                            k_res[s].did_read(o)
                            q_res[s].did_read(o)
                            bres[sbk].did_write(o)
                            qk_ops[kb] = (o, sbk, c0)

                        def make_exp(kb):
                            o_qk, sbk, c0 = qk_ops[kb]
                            pi = CT["pb_ctr"] % NPB
                            CT["pb_ctr"] += 1
                            o = P.add("act", lambda e, sbk=sbk, c0=c0, pi=pi, kb=kb: e.activation(
                                out=Pb[pi][:, c0:TS], in_=banks[sbk][:, c0:TS], func=AF.Exp,
                                bias=Ctok[:, kb * 8 + h:kb * 8 + h + 1], scale=1.0),
                                bres[sbk].rd() + Pb_res[pi].wr() + [o_ct])
                            bres[sbk].did_read(o)
                            Pb_res[pi].did_write(o)
                            ex_ops[kb] = (o, pi, c0)

                        def make_pv(kb):
                            o_ex, pi, c0 = ex_ops[kb]

                            def fn(e, kb=kb, pi=pi, c0=c0):
                                e.matmul(banks[ob][:, c0:TS], lhsT=vtok[s][:, kb, :], rhs=Pb[pi][:, c0:TS],
                                         start=(kb == 0), stop=(kb == nkb - 1))
                                return e.matmul(banks[db][:, c0:TS], lhsT=ones_bf[:], rhs=Pb[pi][:, c0:TS],
                                                start=(kb == 0), stop=(kb == nkb - 1))

                            deps = Pb_res[pi].rd() + v_res[s].rd()
                            if kb == 0:
                                deps = deps + bres[ob].wr() + bres[db].wr()
                            o = P.add("pe", fn, deps)
                            Pb_res[pi].did_read(o)
                            v_res[s].did_read(o)
                            bres[ob].did_write(o, fresh=(kb == 0))
                            bres[db].did_write(o, fresh=(kb == 0))
                            return o

                        make_qk(0)
                        if nkb > 1:
                            make_qk(1)
                        last_pv = None
                        for kb in range(nkb):
                            make_exp(kb)
                            last_pv = make_pv(kb)
                            if kb + 2 < nkb:
                                make_qk(kb + 2)
                        rd = rden[rd_i]
                        o_rc = P.add("dve", lambda e, rd=rd, db=db: e.reciprocal(out=rd[:], in_=banks[db][:, :]),
                                     [last_pv] + rden_res[rd_i].wr())
                        bres[db].did_read(o_rc)
                        rden_res[rd_i].did_write(o_rc)
                        o_mx = P.add("dve", lambda e, rd=rd, ob=ob, j=j: e.tensor_tensor(
                            out=mix_v[:, h, j * TS:(j + 1) * TS], in0=banks[ob][:, :], in1=rd[:], op=ALU.mult),
                            [last_pv, o_rc] + mix_res.rs, keep=True)
                        bres[ob].did_read(o_mx)
                        rden_res[rd_i].did_read(o_mx)
                        mix_res.did_write(o_mx, fresh=False)
                    for j in range(NT):
                        fox_qtile(j)

                for h in range(OPTS["nfox"]):
                    fox_head(h)
                fin2 = mod_finish(32, 32 + CT["mod_i"], mod_ops[-1:]) if CT["mod_i"] > 0 else None
                oG1 = oA2 = oG2 = None
                if fin2 is not None:
                    oG1 = P.add("dve", lambda e: e.tensor_tensor(out=G1[:], in0=modT[:, 32:48], in1=gv[:, 1, :], op=ALU.mult), [fin2], keep=True)
                    oA2 = P.add("dve", lambda e: e.scalar_tensor_tensor(out=A2[:], in0=modT[:, 64:80], scalar=1.0, in1=gv[:, 2, :],
                                                                        op0=ALU.add, op1=ALU.mult), [fin2], keep=True)
                    oG2 = P.add("dve", lambda e: e.tensor_tensor(out=G2[:], in0=modT[:, 80:96], in1=gv[:, 3, :], op=ALU.mult), [fin2], keep=True)
                P.emit_block()

            with ExitStack() as sc_s:
                wrot = [sb(sc_s, f"wrot{i}", [128, KC, 128], BF16) for i in range(2)]
                wrot_res = [Res(), Res()]
                wsv = sb(sc_s, "wsv", [128, KC, 128], BF16)
                wsv_res = Res()
                sqT = sb(sc_s, "sqT", [128, 4, S], BF16)
                skT = sb(sc_s, "skT", [128, S], BF16)
                vts = sb(sc_s, "vts", [128, 16, 128], BF16)
                sq_res = Res()
                sk_res = Res()
                vs_res = Res()
                cs = [sb(sc_s, f"cs{i}", [128, TS], F32) for i in range(2)]
                sn = [sb(sc_s, f"sn{i}", [128, TS], F32) for i in range(2)]
                cs_res = [Res(), Res()]
                qb = [sb(sc_s, f"qb{i}", [128, TS], BF16) for i in range(2)]
                qb_res = [Res(), Res()]
                t1 = [sb(sc_s, f"t1_{i}", [128, TS], F32) for i in range(1)]
                t2 = [sb(sc_s, f"t2_{i}", [128, TS], F32) for i in range(1)]
                t_res = [Res()]
                NPS = 6
                Ps = [sb(sc_s, f"Ps{i}", [128, TS], BF16) for i in range(NPS)]
                Ps_res = [Res() for _ in range(NPS)]
                perm_bf = sb(sc_s, "perm_bf", [128, 128], BF16)
                mswa_bf = sb(sc_s, "mswa_bf", [128, 512], BF16)
                sinks_sb = sb(sc_s, "sinks_sb", [128, 8], F32)
                esink = sb(sc_s, "esink", [128, 8], F32)
                esinkB = sb(sc_s, "esinkB", [128, 8, 128], F32)
                rds = [sb(sc_s, f"rds{i}", [128, TS], F32) for i in range(2)]
                rds_res = [Res(), Res()]
                ld_pm = dma("sp", perm_bf[:], k_perm, [], "ld_small0")
                ld_ms = dma("sp", mswa_bf[:], k_maskswa, [], "ld_small1")
                ld_sk = dma("sp", sinks_sb[:], sinksB, [], "ld_small2")
                o_es = P.add("act", lambda e: e.activation(out=esink[:], in_=sinks_sb[:], func=AF.Exp), [ld_sk])

                def fn_esb(e):
                    ins = None
                    for hh in range(8):
                        ins = e.activation(out=esinkB[:, hh, :], in_=ident_bf[:], func=AF.Identity,
                                           bias=esink[:, hh:hh + 1], scale=0.0)
                    return ins

                o_esb = P.add("act", fn_esb, [o_es, ld_id], keep=True)
                CT["ps_ctr"] = 0
                CT["rp_ctr"] = 0
                CT["qblk_ctr"] = 0
                CT["wrot_ctr"] = 0
                CT["cs_ctr"] = 0
                def swa_group(g):
                    ld = dma("pool", wsv[:], w_in_v[:, :, C_SV + g * 128:C_SV + (g + 1) * 128], wsv_res.wr(), "wsv")
                    wsv_res.did_write(ld)
                    def load_rot(gg, pp):
                        wi = (gg * 5 + pp) % 2
                        if pp < 4:
                            c0w = C_SQ + (4 * gg + pp) * 128
                        else:
                            c0w = C_SK + gg * 128
                        ld = dma("pool", wrot[wi][:], w_in_v[:, :, c0w:c0w + 128], wrot_res[wi].wr(), f"wrot{wi}")
                        wrot_res[wi].did_write(ld)

                    if g == 0:
                        load_rot(0, 0)
                    for pi5 in range(5):
                        wi = (g * 5 + pi5) % 2
                        wt, wres = wrot[wi], wrot_res[wi]
                        nxt = g * 5 + pi5 + 1
                        if nxt < 5 * (OPTS["nswa"] if OPTS["swa"] else 0):
                            load_rot(nxt // 5, nxt % 5)
                        for t in range(NT):
                            ci = CT["cs_ctr"] % 2
                            CT["cs_ctr"] += 1
                            ldc = dma("sp", cs[ci][:], k_cos[:, t * TS:(t + 1) * TS], cs_res[ci].wr(), f"xs{ci}")
                            lds = dma("sp", sn[ci][:], k_sin[:, t * TS:(t + 1) * TS], cs_res[ci].wr(), f"wm{ci}")
                            cs_res[ci].did_write(ldc)
                            cs_res[ci].did_write(lds, fresh=False)
                            ri = CT["rp_ctr"] % 2
                            CT["rp_ctr"] += 1
                            bkA = next_pbank()

                            def fn_p(e, wt=wt, bkA=bkA, t=t):
                                ins = None
                                for k in range(KC):
                                    ins = e.matmul(banks[bkA][:, :], lhsT=wt[:, k, :], rhs=hT[:, k, t * TS:(t + 1) * TS],
                                                   start=(k == 0), stop=(k == KC - 1))
                                return ins

                            o_p = P.add("pe", fn_p, wres.rd() + hT_res.rd() + bres[bkA].wr())
                            wres.did_read(o_p)
                            bres[bkA].did_write(o_p)
                            if pi5 < 4:
                                dst = sqT[:, pi5, t * TS:(t + 1) * TS]
                                dres = sq_res
                            else:
                                dst = skT[:, t * TS:(t + 1) * TS]
                                dres = sk_res
                            fresh = (t == 0 and pi5 in (0, 4))
                            if OPTS["rope"] == 0:
                                o_cp = P.add("dve", lambda e, bkA=bkA, dst=dst: e.tensor_copy(out=dst, in_=banks[bkA][:, :]),
                                             bres[bkA].rd() + (dres.wr() if fresh else dres.rs))
                                bres[bkA].did_read(o_cp)
                                dres.did_write(o_cp, fresh=fresh)
                                continue
                            o_qb = P.add("act", lambda e, ri=ri, bkA=bkA: e.activation(out=qb[ri][:], in_=banks[bkA][:, :], func=AF.Copy),
                                         bres[bkA].rd() + qb_res[ri].wr())
                            bres[bkA].did_read(o_qb)
                            qb_res[ri].did_write(o_qb)
                            bkB = next_pbank()
                            o_pm = P.add("pe", lambda e, ri=ri, bkB=bkB: e.matmul(banks[bkB][:, :], lhsT=perm_bf[:], rhs=qb[ri][:],
                                                                                    start=True, stop=True),
                                         qb_res[ri].rd() + bres[bkB].wr() + [ld_pm])
                            qb_res[ri].did_read(o_pm)
                            bres[bkB].did_write(o_pm)
                            if OPTS["rope"] == 1:
                                o_cp = P.add("dve", lambda e, bkB=bkB, dst=dst: e.tensor_copy(out=dst, in_=banks[bkB][:, :]),
                                             bres[bkB].rd() + bres[bkA].rd() + (dres.wr() if fresh else dres.rs))
                                bres[bkB].did_read(o_cp)
                                bres[bkA].did_read(o_cp)
                                dres.did_write(o_cp, fresh=fresh)
                                continue
                            o_t1 = P.add("dve", lambda e, bkA=bkA, ci=ci: e.tensor_tensor(
                                out=t1[0][:], in0=banks[bkA][:, :], in1=cs[ci][:], op=ALU.mult),
                                bres[bkA].rd() + cs_res[ci].rd() + t_res[0].wr() + [o_qb])
                            bres[bkA].did_read(o_t1)
                            cs_res[ci].did_read(o_t1)
                            o_t2 = P.add("dve", lambda e, bkB=bkB, ci=ci: e.tensor_tensor(
                                out=t2[0][:], in0=banks[bkB][:, :], in1=sn[ci][:], op=ALU.mult),
                                bres[bkB].rd() + cs_res[ci].rd() + t_res[0].wr())
                            bres[bkB].did_read(o_t2)
                            cs_res[ci].did_read(o_t2)
                            t_res[0].did_write(o_t1)
                            t_res[0].did_write(o_t2, fresh=False)
                            if pi5 < 4:
                                dst = sqT[:, pi5, t * TS:(t + 1) * TS]
                                dres = sq_res
                            else:
                                dst = skT[:, t * TS:(t + 1) * TS]
                                dres = sk_res
                            fresh = (t == 0 and pi5 in (0, 4))
                            if OPTS["rope"] == 2:
                                o_cp = P.add("dve", lambda e, dst=dst: e.tensor_copy(out=dst, in_=t1[0][:]),
                                             t_res[0].rd() + (dres.wr() if fresh else dres.rs))
                                t_res[0].did_read(o_cp)
                                dres.did_write(o_cp, fresh=fresh)
                                continue
                            o_ad = P.add("dve", lambda e, dst=dst: e.tensor_tensor(out=dst, in0=t1[0][:], in1=t2[0][:], op=ALU.add),
                                         t_res[0].rd() + (dres.wr() if fresh else dres.rs))
                            t_res[0].did_read(o_ad)
                            dres.did_write(o_ad, fresh=fresh)
                    for t in range(NT):
                        proj_v_tok(wsv, wsv_res, t, vts, vs_res, fresh=(t == 0))
                    if OPTS["swa_lvl"] < 2:
                        return
                    Pinfo = {}

                    def make_sqk(kb, pair):
                        sbk = next_pbank()
                        W = 256 if kb < 15 else 128

                        def fn(e, kb=kb, pair=pair, sbk=sbk, W=W):
                            outv = banks[sbk][:, 0:2 * W].rearrange("p (a w) -> p a w", a=2)
                            e.matmul(outv, lhsT=skT[:, kb * 128:(kb + 1) * 128],
                                     rhs=sqT[:, 2 * pair:2 * pair + 2, kb * 128:kb * 128 + W], start=True, stop=False)
                            return e.matmul(outv, lhsT=ident_bf[:],
                                            rhs=mswa_bf[:].rearrange("p (a w) -> p a w", a=2)[:, :, 0:W], start=False, stop=True)

                        o = P.add("pe", fn, sk_res.rd() + sq_res.rd() + bres[sbk].wr() + [ld_ms, ld_id])
                        sk_res.did_read(o)
                        sq_res.did_read(o)
                        bres[sbk].did_write(o)
                        return o, sbk, W

                    def make_sexp(kb, pair, qk):
                        o_qk, sbk, W = qk
                        pi = CT["ps_ctr"] % NPS
                        CT["ps_ctr"] += 1
                        o = P.add("act", lambda e, sbk=sbk, W=W, pi=pi: e.activation(
                            out=Ps[pi][:, 0:2 * W], in_=banks[sbk][:, 0:2 * W], func=AF.Exp, scale=SCALE),
                            bres[sbk].rd() + Ps_res[pi].wr())
                        bres[sbk].did_read(o)
                        Ps_res[pi].did_write(o)
                        Pinfo[(kb, pair)] = (pi, W)

                    def make_spv(qbk):
                        ob = 3 + 2 * (CT["qblk_ctr"] % 2)
                        db = ob + 1
                        ri = CT["qblk_ctr"] % 2
                        CT["qblk_ctr"] += 1

                        def fn(e, qbk=qbk, ob=ob, db=db):
                            ins = None
                            for (bank_i, is_den) in ((ob, False), (db, True)):
                                for pair in range(2):
                                    outv = banks[bank_i][:, pair * 256:(pair + 1) * 256].rearrange("p (a w) -> p a w", a=2)
                                    first = True
                                    if qbk > 0:
                                        pi, W = Pinfo[(qbk - 1, pair)]
                                        rhs = Ps[pi][:, 0:2 * W].rearrange("p (a w) -> p a w", a=2)[:, :, 128:256]
                                        lhsT = ones_bf[:] if is_den else vts[:, qbk - 1, :]
                                        ins = e.matmul(outv, lhsT=lhsT, rhs=rhs, start=True, stop=False)
                                        first = False
                                    pi, W = Pinfo[(qbk, pair)]
                                    rhs = Ps[pi][:, 0:2 * W].rearrange("p (a w) -> p a w", a=2)[:, :, 0:128]
                                    lhsT = ones_bf[:] if is_den else vts[:, qbk, :]
                                    ins = e.matmul(outv, lhsT=lhsT, rhs=rhs, start=first, stop=True)
                            return ins

                        deps = vs_res.rd() + bres[ob].wr() + bres[db].wr()
                        used = []
                        for pair in range(2):
                            for kk in ((qbk - 1, qbk) if qbk > 0 else (qbk,)):
                                pi, W = Pinfo[(kk, pair)]
                                deps = deps + Ps_res[pi].rd()
                                used.append(pi)
                        o = P.add("pe", fn, deps)
                        for pi in used:
                            Ps_res[pi].did_read(o)
                        vs_res.did_read(o)
                        bres[ob].did_write(o)
                        bres[db].did_write(o)
                        rd = rds[ri]
                        o_a = P.add("dve", lambda e, rd=rd, db=db: e.tensor_tensor(
                            out=rd[:], in0=banks[db][:, :], in1=esinkB[:, 4 * g:4 * g + 4, :].rearrange("p a w -> p (a w)"), op=ALU.add),
                            [o, o_esb] + rds_res[ri].wr())
                        bres[db].did_read(o_a)
                        o_r = P.add("dve", lambda e, rd=rd: e.reciprocal(out=rd[:], in_=rd[:]), [o_a])
                        rds_res[ri].did_write(o_r)
                        o_m = P.add("dve", lambda e, rd=rd, ob=ob, qbk=qbk: e.tensor_tensor(
                            out=mix_v[:, 8 + 4 * g:8 + 4 * g + 4, qbk * 128:(qbk + 1) * 128],
                            in0=banks[ob][:, :].rearrange("p (a w) -> p a w", a=4),
                            in1=rd[:].rearrange("p (a w) -> p a w", a=4), op=ALU.mult),
                            [o, o_r] + mix_res.rs, keep=True)
                        bres[ob].did_read(o_m)
                        rds_res[ri].did_read(o_m)
                        mix_res.did_write(o_m, fresh=False)

                    pend = [make_sqk(0, 0), make_sqk(0, 1)]
                    for kb in range(16):
                        nxt = []
                        if kb + 1 < 16:
                            nxt = [make_sqk(kb + 1, 0), make_sqk(kb + 1, 1)] if False else []
                        make_sexp(kb, 0, pend[0])
                        make_sexp(kb, 1, pend[1])
                        if kb + 1 < 16:
                            pend_next = [make_sqk(kb + 1, 0), make_sqk(kb + 1, 1)]
                        if OPTS["swa_lvl"] >= 3:
                            make_spv(kb)
                        if kb + 1 < 16:
                            pend = pend_next
                for g in range(OPTS["nswa"] if OPTS["swa"] else 0):
                    swa_group(g)
                if "d_mix" in dbg:
                    dump("d_mix", mixT[:], mix_res.rd())
                P.emit_block()
        if stop == "A":
            P.add("sp", None, final_deps)
            P.emit_block()
            return nc

        with ExitStack() as sc_2:
            ya = sb(sc_2, "ya", [128, KC, TS], F32)
            x1 = sb(sc_2, "x1", [128, KC, TS], F32)
            h2 = sb(sc_2, "h2", [128, KC, TS], BF16)
            abuf = [sb(sc_2, f"abuf{i}", [128, 8, TS], BF16) for i in range(2)]
            NSL = 3
            ring = [sb(sc_2, f"ring{i}", [128, 4096], BF16) for i in range(NSL)]
            ring_res = [Res() for _ in range(NSL)]
            NSQ = 4
            sqc = [sb(sc_2, f"sqc{i}", [128, TS], BF16) for i in range(NSQ)]
            sqc_res = [Res() for _ in range(NSQ)]
            rt = [sb(sc_2, f"rt{i}", [128, TS], F32) for i in range(2)]
            rt_res = [Res(), Res()]
            rtmp = sb(sc_2, "rtmp", [128, TS], F32)
            rtmp_res = Res()
            ya_c = [Res() for _ in range(KC)]
            x1_c = [Res() for _ in range(KC)]
            h2_res = Res()
            a_res = [Res(), Res()]
            CT["ring_ctr"] = 0
            CT["ob_ctr"] = 0
            CT["sq_ctr"] = 0
            CT["rt_ctr"] = 0
            SBK = [6, 7]
            stat_ctr = [0]
            pending = []

            def load_w(src_ap, view, cvk="A"):
                si = CT["ring_ctr"] % NSL
                CT["ring_ctr"] += 1
                if view == 16:
                    dst = ring[si][:].rearrange("p (k n) -> p k n", k=16)
                else:
                    dst = ring[si][:].rearrange("p (k n) -> p k n", k=8)
                ld = dma("sp", dst, src_ap, ring_res[si].wr() + cvdep[cvk], f"ring{si}")
                ring_res[si].did_write(ld)
                return si, dst

            def next_ob():
                b = CT["ob_ctr"] % 6
                CT["ob_ctr"] += 1
                return b

            def stats_add(src_ap, src_deps, sbk, first, last):
                qi = CT["sq_ctr"] % NSQ
                CT["sq_ctr"] += 1
                o_s = P.add("act", lambda e, qi=qi: e.activation(out=sqc[qi][:], in_=src_ap, func=AF.Square),
                            src_deps + sqc_res[qi].wr())
                sqc_res[qi].did_write(o_s)

                def mk(qi=qi, sbk=sbk, first=first, last=last):
                    o_m = P.add("pe", lambda e: e.matmul(banks[sbk][:, :], lhsT=ones_bf[:], rhs=sqc[qi][:], start=first, stop=last),
                                sqc_res[qi].rd() + (bres[sbk].wr() if first else []))
                    sqc_res[qi].did_read(o_m)
                    bres[sbk].did_write(o_m, fresh=first)
                    return o_m

                pending.append(mk)
                return o_s

            last_stat = [None]

            def flush_stats(keep):
                while len(pending) > keep:
                    last_stat[0] = pending.pop(0)()

            def make_rstd(sbk):
                flush_stats(0)
                o1 = P.add("act", lambda e: e.activation(out=rtmp[:], in_=banks[sbk][:, :], func=AF.Sqrt, bias=EPS, scale=1.0 / D),
                           [last_stat[0]] + bres[sbk].rd() + rtmp_res.wr())
                bres[sbk].did_read(o1)
                rtmp_res.did_write(o1)
                o2 = P.add("dve", lambda e: e.reciprocal(out=banks[sbk][:, :], in_=rtmp[:]), [o1] + bres[sbk].wr())
                rtmp_res.did_read(o2)
                bres[sbk].did_write(o2)
                return o2

            out_stores = []
            cvdep = {}

            def p2_tile(j):
                tsl = slice(j * TS, (j + 1) * TS)
                for q in range(4):
                    deps = []
                    for m in range(4 * q, 4 * q + 4):
                        deps += x1_c[m].wr()
                    ldx = dma("pool", x1[:, 4 * q:4 * q + 4, :], xT_v[:, 4 * q:4 * q + 4, tsl], deps, f"ldx{q}")
                    for m in range(4 * q, 4 * q + 4):
                        x1_c[m].did_write(ldx)
                if j == 0:
                    issue_conv(max(0, NCVA - len(conv_ops)))
                    cvdep["A"] = [conv_ops[min(NCVA, len(conv_ops)) - 1]]
                    issue_conv(len(conv_jobs))
                    cvdep["B"] = [conv_ops[-1]]
                sb1 = SBK[stat_ctr[0] % 2]
                stat_ctr[0] += 1
                for oc in range(8):
                    si, wv16 = load_w(wo_b[oc], 16)
                    for mm in range(2):
                        m = 2 * oc + mm
                        bk = next_ob()

                        def fn(e, wv16=wv16, mm=mm, bk=bk):
                            ins = None
                            for k in range(KC):
                                ins = e.matmul(banks[bk][:, :], lhsT=wv16[:, k, mm * 128:(mm + 1) * 128], rhs=mix_v[:, k, tsl],
                                               start=(k == 0), stop=(k == KC - 1))
                            return ins

                        o = P.add("pe", fn, ring_res[si].rd() + mix_res.rd() + bres[bk].wr())
                        ring_res[si].did_read(o)
                        mix_res.did_read(o)
                        bres[bk].did_write(o)
                        flush_stats(2)
                        ev = P.add("dve", lambda e, bk=bk, m=m: e.tensor_copy(out=ya[:, m, :], in_=banks[bk][:, :]),
                                   bres[bk].rd() + ya_c[m].wr())
                        bres[bk].did_read(ev)
                        ya_c[m].did_write(ev)
                        o_s = stats_add(ya[:, m, :], [ev], sb1, first=(m == 0), last=(m == KC - 1))
                        ya_c[m].did_read(o_s)
                o_rs = make_rstd(sb1)
                sb2 = SBK[stat_ctr[0] % 2]
                stat_ctr[0] += 1
                for m in range(KC):
                    o1 = P.add("dve", lambda e, m=m: e.scalar_tensor_tensor(out=ya[:, m, :], in0=ya[:, m, :], scalar=G1[:, m:m + 1],
                                                                            in1=banks[sb1][:, :], op0=ALU.mult, op1=ALU.mult),
                               ya_c[m].wr() + [o_rs, oG1])
                    bres[sb1].did_read(o1)
                    ya_c[m].did_write(o1)
                    o2 = P.add("dve", lambda e, m=m: e.tensor_tensor(out=x1[:, m, :], in0=x1[:, m, :], in1=ya[:, m, :], op=ALU.add),
                               [o1] + x1_c[m].wr())
                    ya_c[m].did_read(o2)
                    x1_c[m].did_write(o2)
                    o_s = stats_add(x1[:, m, :], [o2], sb2, first=(m == 0), last=(m == KC - 1))
                    x1_c[m].did_read(o_s)
                    flush_stats(0)
                o_rs2 = make_rstd(sb2)
                h2_ops = []
                for m in range(KC):
                    o3 = P.add("dve", lambda e, m=m: e.tensor_tensor(out=ya[:, m, :], in0=x1[:, m, :], in1=banks[sb2][:, :], op=ALU.mult),
                               ya_c[m].wr() + x1_c[m].rd() + [o_rs2])
                    bres[sb2].did_read(o3)
                    x1_c[m].did_read(o3)
                    ya_c[m].did_write(o3)
                    o4 = P.add("act", lambda e, m=m: e.activation(out=h2[:, m, :], in_=ya[:, m, :], func=AF.Identity,
                                                                  bias=modT[:, 48 + m:48 + m + 1], scale=A2[:, m:m + 1]),
                               [o3, oA2] + (h2_res.wr() if m == 0 else h2_res.rs))
                    ya_c[m].did_read(o4)
                    h2_res.did_write(o4, fresh=(m == 0))
                sb3 = SBK[stat_ctr[0] % 2]
                stat_ctr[0] += 1
                def mlp_up(g):
                    ai = g % 2
                    for ut in range(4):
                        si, wv16 = load_w(wu_b[g * 4 + ut], 16, "A" if g < 4 else "B")
                        for mm in range(2):
                            fc = 2 * ut + mm
                            bk = next_ob()

                            def fn(e, wv16=wv16, mm=mm, bk=bk):
                                ins = None
                                for k in range(KC):
                                    ins = e.matmul(banks[bk][:, :], lhsT=wv16[:, k, mm * 128:(mm + 1) * 128], rhs=h2[:, k, :],
                                                   start=(k == 0), stop=(k == KC - 1))
                                return ins

                            o = P.add("pe", fn, ring_res[si].rd() + h2_res.rd() + bres[bk].wr())
                            ring_res[si].did_read(o)
                            h2_res.did_read(o)
                            bres[bk].did_write(o)
                            flush_stats(2)
                            ri = CT["rt_ctr"] % 2
                            CT["rt_ctr"] += 1
                            o_r = P.add("act", lambda e, ri=ri, bk=bk: e.activation(out=rt[ri][:], in_=banks[bk][:, :], func=AF.Relu),
                                        bres[bk].rd() + rt_res[ri].wr())
                            bres[bk].did_read(o_r)
                            rt_res[ri].did_write(o_r)
                            o_a = P.add("dve", lambda e, ri=ri, bk=bk, ai=ai, fc=fc: e.scalar_tensor_tensor(
                                out=abuf[ai][:, fc, :], in0=banks[bk][:, :], scalar=0.0, in1=rt[ri][:], op0=ALU.max, op1=ALU.mult),
                                bres[bk].rd() + rt_res[ri].rd() + (a_res[ai].wr() if fc == 0 else a_res[ai].rs))
                            bres[bk].did_read(o_a)
                            rt_res[ri].did_read(o_a)
                            a_res[ai].did_write(o_a, fresh=(fc == 0))

                def mlp_down(g):
                    ai = g % 2
                    for dt in range(4):
                        si, wv8 = load_w(wd_b[g * 4 + dt], 8, "A" if g < 4 else "B")
                        for mm in range(4):
                            m = 4 * dt + mm
                            bk = next_ob()

                            def fn(e, wv8=wv8, mm=mm, bk=bk, ai=ai):
                                ins = None
                                for fc in range(8):
                                    ins = e.matmul(banks[bk][:, :], lhsT=wv8[:, fc, mm * 128:(mm + 1) * 128], rhs=abuf[ai][:, fc, :],
                                                   start=(fc == 0), stop=(fc == 7))
                                return ins

                            o = P.add("pe", fn, ring_res[si].rd() + a_res[ai].rd() + bres[bk].wr())
                            ring_res[si].did_read(o)
                            a_res[ai].did_read(o)
                            bres[bk].did_write(o)
                            flush_stats(2)
                            if g == 0:
                                ev = P.add("dve", lambda e, bk=bk, m=m: e.tensor_copy(out=ya[:, m, :], in_=banks[bk][:, :]),
                                           bres[bk].rd() + ya_c[m].wr())
                            else:
                                ev = P.add("dve", lambda e, bk=bk, m=m: e.tensor_tensor(out=ya[:, m, :], in0=banks[bk][:, :], in1=ya[:, m, :],
                                                                                         op=ALU.add),
                                           bres[bk].rd() + ya_c[m].wr())
                            ya_c[m].did_write(ev)
                            bres[bk].did_read(ev)
                            if g == 7:
                                o_s = stats_add(ya[:, m, :], [ev], sb3, first=(m == 0), last=(m == KC - 1))
                                ya_c[m].did_read(o_s)

                mlp_up(0)
                for g in range(8):
                    if g + 1 < 8:
                        mlp_up(g + 1)
                    mlp_down(g)
                o_rs3 = make_rstd(sb3)
                for q in range(4):
                    fin = []
                    for m in range(4 * q, 4 * q + 4):
                        o5 = P.add("dve", lambda e, m=m: e.scalar_tensor_tensor(out=ya[:, m, :], in0=ya[:, m, :], scalar=G2[:, m:m + 1],
                                                                                in1=banks[sb3][:, :], op0=ALU.mult, op1=ALU.mult),
                                   ya_c[m].wr() + [o_rs3, oG2])
                        bres[sb3].did_read(o5)
                        ya_c[m].did_write(o5)
                        o6 = P.add("dve", lambda e, m=m: e.tensor_tensor(out=x1[:, m, :], in0=x1[:, m, :], in1=ya[:, m, :], op=ALU.add),
                                   [o5] + x1_c[m].wr())
                        ya_c[m].did_read(o6)
                        x1_c[m].did_write(o6)
                        fin.append(o6)
                    st = dma("pool", outT_v[:, 4 * q:4 * q + 4, tsl], x1[:, 4 * q:4 * q + 4, :], fin, f"st{q}")
                    for m in range(4 * q, 4 * q + 4):
                        x1_c[m].did_read(st)
                    out_stores.append(st)

            for j in range(NT):
                p2_tile(j)
            P.add("sp", None, out_stores + final_deps)
            P.emit_block()
    return nc


def _host_inputs(inputs):
    f32 = np.float32
    x = np.asarray(inputs["x"], f32)
    c = np.asarray(inputs["c"], f32)
    consts = _consts()
    shared = {
        "w_mod": np.ascontiguousarray(np.asarray(inputs["w_mod"], f32)[0]),
        "b_modT": np.ascontiguousarray(np.asarray(inputs["b_mod"], f32)[0].reshape(96, 128).T),
        "gvec": np.ascontiguousarray(np.stack([np.asarray(inputs[k], f32)[0].reshape(KC, 128).T for k in
                                               ("g_pre_mix", "g_post_mix", "g_pre_mlp", "g_post_mlp")], axis=1)),
        "w_in": np.ascontiguousarray(np.asarray(inputs["w_in"], f32)[0]),
        "b_fg": np.ascontiguousarray(np.asarray(inputs["b_forget"], f32)[0].reshape(8, 1)),
        "sinksB": np.ascontiguousarray(np.broadcast_to(np.asarray(inputs["swa_sinks"], f32)[0][None, :], (128, 8))),
        "w_out": np.ascontiguousarray(np.asarray(inputs["w_out"], f32)[0]),
        "w_up": np.ascontiguousarray(np.asarray(inputs["w_up"], f32)[0]),
        "w_down": np.ascontiguousarray(np.asarray(inputs["w_down"], f32)[0]),
    }
    shared.update(consts)
    in_maps = []
    for b in range(8):
        m = dict(shared)
        m["xT"] = np.ascontiguousarray(x[b].T)
        m["c_col"] = np.ascontiguousarray(c[b].reshape(KC, 128).T)
        in_maps.append(m)
    return in_maps


def kernel(**inputs):
    in_maps = _host_inputs(inputs)
    nc = build("full")
    res = run_bass_kernel_spmd(nc, in_maps, core_ids=list(range(8)))
    out = np.stack([np.ascontiguousarray(r["outT"].T) for r in res.results], axis=0)
    return out.astype(np.float32)
```

```python
import numpy as np
import ml_dtypes
import concourse.bass as bass
import concourse.mybir as mybir
from concourse.bass_utils import run_bass_kernel_spmd

F32 = mybir.dt.float32
BF16 = mybir.dt.bfloat16
AF = mybir.ActivationFunctionType
ALU = mybir.AluOpType

D = 2048
S = 2048
KC = 16
NT = 4
TS = 512
DFF = 8192
NMOD = 6 * D
INW = 4616
EPS = 1e-6
SCALE = 128 ** -0.5
C_FQ, C_FK, C_FV, C_FG, C_SQ, C_SK, C_SV = 0, 1024, 2048, 3072, 3080, 4104, 4360
MASKV = -30000.0

ENG_ATTR = {"pe": "tensor", "act": "scalar", "dve": "vector", "pool": "gpsimd", "sp": "sync"}


class Op:
    __slots__ = ("eng", "fn", "deps", "dsem", "sig", "ncons", "blk", "keep")


class Res:
    def __init__(self):
        self.ws = []
        self.rs = []

    def rd(self):
        return list(self.ws)

    def wr(self):
        return list(self.ws) + list(self.rs)

    def did_read(self, op):
        self.rs.append(op)

    def did_write(self, op, fresh=True):
        if fresh:
            self.ws = [op]
            self.rs = []
        else:
            self.ws.append(op)


class Prog:
    def __init__(self, nc, stack):
        self.nc = nc
        self.stack = stack
        self.ops = []
        self.sem = {}
        self.cnt = {}
        self.waited = {e: {} for e in ENG_ATTR}
        self.blk = 0
        for e in ("pe", "act", "dve", "pool"):
            self._sem("c_" + e)

    def _sem(self, name):
        if name not in self.sem:
            self.sem[name] = self.stack.enter_context(self.nc.semaphore(name))
            self.cnt[name] = 0
        return self.sem[name]

    def add(self, eng, fn, deps=(), dsem=None, keep=False):
        op = Op()
        op.eng = eng
        op.fn = fn
        op.deps = [d for d in deps if d is not None]
        op.dsem = dsem
        op.sig = None
        op.ncons = 0
        op.blk = self.blk
        op.keep = keep
        for d in op.deps:
            d.ncons += 1
        if dsem is not None:
            self._sem(dsem)
        self.ops.append(op)
        return op

    def emit_block(self):
        nc = self.nc
        ops = self.ops
        self.ops = []
        for op in ops:
            if op.dsem is not None:
                self.cnt[op.dsem] += 16
                op.sig = (op.dsem, self.cnt[op.dsem])
            elif op.fn is not None and (op.ncons > 0 or op.keep):
                nm = "c_" + op.eng
                self.cnt[nm] += 1
                op.sig = (nm, self.cnt[nm])
        cur = self.blk
        with nc.Block() as block:
            for eng, attr in ENG_ATTR.items():
                mine = [op for op in ops if op.eng == eng]
                if not mine:
                    continue

                def body(e, mine=mine, eng=eng):
                    waited = self.waited[eng]
                    for op in mine:
                        for d in op.deps:
                            if d.blk != cur and d.dsem is None:
                                continue
                            sname, val = d.sig
                            if waited.get(sname, 0) < val:
                                e.wait_ge(self.sem[sname], val)
                                waited[sname] = val
                        if op.fn is not None:
                            ins = op.fn(e)
                            if op.sig is not None:
                                ins.then_inc(self.sem[op.sig[0]], 16 if op.dsem is not None else 1)

                getattr(block, attr)(body)
        self.blk += 1


def _consts():
    bf = ml_dtypes.bfloat16
    ident = np.eye(128, dtype=np.float32)
    perm = np.zeros((128, 128), np.float32)
    for m in range(128):
        perm[(m + 64) % 128, m] = 1.0
    kk = np.arange(128)[:, None]
    qq = np.arange(128)[None, :]
    maskc = np.where(kk <= qq, 0.0, MASKV).astype(np.float32)
    mask2 = np.where(kk > qq, 0.0, MASKV).astype(np.float32)
    maskswa = np.concatenate([maskc, mask2, maskc, mask2], axis=1)
    sel = np.zeros((128, 8, 128), np.float32)
    for h in range(8):
        for r in (h, 32 + h, 64 + h):
            sel[r, h, :] = 1.0
    half = 64
    inv = (1.0 / (10000.0 ** (np.arange(half, dtype=np.float32) * (2.0 / 128)))).astype(np.float32)
    ang = (np.arange(S, dtype=np.float32)[None, :] * inv[:, None]).astype(np.float32)
    cos = np.cos(ang).astype(np.float32)
    sin = np.sin(ang).astype(np.float32)
    cosT = np.concatenate([cos, cos], axis=0)
    sinT = np.concatenate([-sin, sin], axis=0)
    return {
        "k_ident": ident.astype(bf), "k_perm": perm.astype(bf), "k_maskc": maskc.astype(bf),
        "k_maskswa": maskswa.astype(bf), "k_sel": sel.astype(bf), "k_cos": cosT, "k_sin": sinT,
        "k_identf": np.eye(8, dtype=np.float32),
    }


OPTS = {"nfox": 8, "mod_bg": True, "swa": True, "nswa": 2, "swa_lvl": 9, "rope": 9}


def build(stop="full", debug=()):
    nc = bass.Bass("TRN2", target_bir_lowering=False)
    from contextlib import ExitStack

    def din(name, shape, dt=F32):
        return nc.dram_tensor(name, list(shape), dt, kind="ExternalInput").ap()

    xT = din("xT", [D, S])
    c_col = din("c_col", [128, KC])
    w_mod = din("w_mod", [D, NMOD])
    b_modT = din("b_modT", [128, 96])
    gvec = din("gvec", [128, 4, KC])
    w_in = din("w_in", [D, INW])
    b_fg = din("b_fg", [8, 1])
    sinksB = din("sinksB", [128, 8])
    w_out = din("w_out", [D, D])
    w_up = din("w_up", [D, DFF])
    w_down = din("w_down", [DFF, D])
    k_ident = din("k_ident", [128, 128], BF16)
    k_perm = din("k_perm", [128, 128], BF16)
    k_maskc = din("k_maskc", [128, 128], BF16)
    k_maskswa = din("k_maskswa", [128, 512], BF16)
    k_sel = din("k_sel", [128, 8, 128], BF16)
    k_cos = din("k_cos", [128, S])
    k_sin = din("k_sin", [128, S])
    k_identf = din("k_identf", [8, 8])
    outT = nc.dram_tensor("outT", [D, S], F32, kind="ExternalOutput").ap()
    dbg = {}
    for nm, shape, dt in debug:
        dbg[nm] = nc.dram_tensor(nm, list(shape), dt, kind="ExternalOutput").ap()

    xT_v = xT.rearrange("(k p) s -> p k s", p=128)
    outT_v = outT.rearrange("(k p) s -> p k s", p=128)
    w_mod_v = w_mod.rearrange("(k p) n -> p k n", p=128)
    w_in_v = w_in.rearrange("(k p) n -> p k n", p=128)
    w_out_v = w_out.rearrange("(k p) n -> p k n", p=128)
    w_up_v = w_up.rearrange("(k p) n -> p k n", p=128)
    w_down_v = w_down.rearrange("(k p) n -> p k n", p=128)
    wo_b = nc.dram_tensor("wo_b", [8, 128, 16, 256], BF16).ap()
    wu_b = nc.dram_tensor("wu_b", [32, 128, 16, 256], BF16).ap()
    wd_b = nc.dram_tensor("wd_b", [32, 128, 8, 512], BF16).ap()
    conv_jobs = []
    for oc in range(8):
        conv_jobs.append((wo_b[oc], w_out_v[:, :, oc * 256:(oc + 1) * 256]))
    for g in range(8):
        for ut in range(4):
            c0 = g * 1024 + ut * 256
            conv_jobs.append((wu_b[g * 4 + ut], w_up_v[:, :, c0:c0 + 256]))
        for dt in range(4):
            conv_jobs.append((wd_b[g * 4 + dt], w_down_v[:, g * 8:(g + 1) * 8, dt * 512:(dt + 1) * 512]))
    conv_ops = []

    with ExitStack() as top:
        P = Prog(nc, top)

        def sb(stack, name, shape, dt):
            return stack.enter_context(nc.sbuf_tensor(name, list(shape), dt))

        banks = [top.enter_context(nc.psum_tensor(f"bank{i}", [128, 512], F32)) for i in range(8)]
        bres = [Res() for _ in range(8)]
        mixT = sb(top, "mixT", [128, KC * S], BF16)
        mix_v = mixT[:].rearrange("p (k s) -> p k s", k=KC)
        mix_res = Res()
        ones_bf = sb(top, "ones_bf", [128, 128], BF16)
        ident_bf = sb(top, "ident_bf", [128, 128], BF16)
        modT = sb(top, "modT", [128, 96], F32)
        bmod_sb = sb(top, "bmod_sb", [128, 96], F32)
        gv = sb(top, "gv", [128, 4, KC], F32)
        A1 = sb(top, "A1", [128, KC], F32)
        G1 = sb(top, "G1", [128, KC], F32)
        A2 = sb(top, "A2", [128, KC], F32)
        G2 = sb(top, "G2", [128, KC], F32)
        cc = sb(top, "cc", [128, KC], F32)
        cond_bf = sb(top, "cond_bf", [128, KC], BF16)

        final_deps = []
        CT = {k: 0 for k in ['pb_ctr', 'sbank_ctr', 'jt_ctr', 'ps_ctr', 'rp_ctr', 'qblk_ctr', 'wrot_ctr', 'cs_ctr', 'ring_ctr', 'ob_ctr', 'sq_ctr', 'rt_ctr', 'mod_i']}

        def dma(eng, out, in_, deps, dsem):
            return P.add(eng, lambda e: e.dma_start(out=out, in_=in_), deps, dsem=dsem)

        NCVA = 40

        def issue_conv(n):
            for _ in range(n):
                if conv_jobs:
                    dst, srcap = conv_jobs.pop(0)
                    conv_ops.append(dma("pool", dst, srcap, [], "cvA" if len(conv_ops) < NCVA else "cvB"))

        def dump(name, src_ap, deps):
            if name in dbg:
                op = dma("sp", dbg[name], src_ap, deps, dsem="dbg_" + name)
                final_deps.append(op)

        ld_c = dma("sp", cc[:], c_col, [], "ld_small0")
        ld_bm = dma("sp", bmod_sb[:], b_modT, [], "ld_small1")
        ld_gv = dma("sp", gv[:], gvec, [], "ld_small2")
        ld_id = dma("sp", ident_bf[:], k_ident, [], "ld_small3")
        op_ones = P.add("dve", lambda e: e.memset(ones_bf[:], 1.0), [], keep=True)
        op_cond = P.add("act", lambda e: e.activation(out=cond_bf[:], in_=cc[:], func=AF.Silu), [ld_c], keep=True)

        MB = 7

        def mod_dma(wm, wm_res, slot, c0, ncols):
            w = wm[slot]
            r = wm_res[slot]
            ld = dma("pool", w[:, :, 0:ncols], w_mod_v[:, :, c0:c0 + ncols], r.wr(), f"wm{slot}")
            r.did_write(ld)
            return ld

        def mod_pe(wm, wm_res, slot, c0, ncols):
            w = wm[slot]
            r = wm_res[slot]

            def fn(e):
                ins = None
                for jj in range(ncols // 128):
                    j = c0 // 128 + jj
                    for k in range(KC):
                        ins = e.matmul(banks[MB][:, j:j + 1], lhsT=w[:, k, jj * 128:(jj + 1) * 128],
                                       rhs=cond_bf[:, k:k + 1], start=(k == 0), stop=(k == KC - 1))
                return ins

            mm = P.add("pe", fn, r.rd() + [op_cond] + bres[MB].wr())
            r.did_read(mm)
            bres[MB].did_write(mm, fresh=False)
            return mm

        def mod_tile(wm, wm_res, slot, c0, ncols):
            mod_dma(wm, wm_res, slot, c0, ncols)
            return mod_pe(wm, wm_res, slot, c0, ncols)

        def mod_finish(j0, j1, deps):
            op = P.add("dve", lambda e: e.tensor_tensor(out=modT[:, j0:j1], in0=banks[MB][:, j0:j1],
                                                        in1=bmod_sb[:, j0:j1], op=ALU.add),
                       deps + [ld_bm], keep=True)
            bres[MB].did_read(op)
            return op

        with ExitStack() as sc_h:
            hT = sb(sc_h, "hT", [128, KC, S], BF16)
            hT_res = Res()
            with ExitStack() as sc_n:
                wm = [sb(sc_n, f"wmN{i}", [128, KC, 512], BF16) for i in range(2)]
                wm_res = [Res(), Res()]
                sq = sb(sc_n, "sqN", [128, KC, TS], BF16)
                sq_res = Res()
                rstd = [sb(sc_n, f"rstdN{i}", [128, TS], F32) for i in range(2)]
                rstd_res = [Res(), Res()]
                xs_all = mixT[:].bitcast(F32).rearrange("p (i k s) -> p i k s", i=2, k=KC)
                xs_res = [Res(), Res()]

                def norm_stage1(t):
                    i = t % 2
                    xs = xs_all[:, i]
                    ld = dma("sp", xs, xT_v[:, :, t * TS:(t + 1) * TS], xs_res[i].wr(), f"xs{i}")
                    xs_res[i].did_write(ld)
                    o_sq = P.add("act", lambda e, xs=xs: e.activation(out=sq[:], in_=xs, func=AF.Square),
                                 xs_res[i].rd() + sq_res.wr())
                    xs_res[i].did_read(o_sq)
                    sq_res.did_write(o_sq)
                    bk = t % 2

                    def fn_ss(e, bk=bk):
                        ins = None
                        for k in range(KC):
                            ins = e.matmul(banks[bk][:, :], lhsT=ones_bf[:], rhs=sq[:, k, :],
                                           start=(k == 0), stop=(k == KC - 1))
                        return ins

                    o_ss = P.add("pe", fn_ss, sq_res.rd() + bres[bk].wr() + [op_ones])
                    sq_res.did_read(o_ss)
                    bres[bk].did_write(o_ss)
                    rs = rstd[i]
                    o_sqrt = P.add("act", lambda e, rs=rs, bk=bk: e.activation(out=rs[:], in_=banks[bk][:, :], func=AF.Sqrt,
                                                                                 bias=EPS, scale=1.0 / D),
                                   bres[bk].rd() + rstd_res[i].wr())
                    bres[bk].did_read(o_sqrt)
                    rstd_res[i].did_write(o_sqrt)
                    o_rec = P.add("dve", lambda e, rs=rs: e.reciprocal(out=rs[:], in_=rs[:]), rstd_res[i].rd())
                    rstd_res[i].did_write(o_rec)
                    o_mul = P.add("dve", lambda e, xs=xs, rs=rs: e.tensor_tensor(
                        out=xs, in0=xs, in1=rs[:].unsqueeze(1).broadcast_to([128, KC, TS]), op=ALU.mult),
                        rstd_res[i].rd() + xs_res[i].wr())
                    rstd_res[i].did_read(o_mul)
                    xs_res[i].did_write(o_mul)

                def norm_stage2(t, opA1):
                    i = t % 2
                    xs = xs_all[:, i]

                    def fn_h(e, xs=xs, t=t):
                        ins = None
                        for k in range(KC):
                            ins = e.activation(out=hT[:, k, t * TS:(t + 1) * TS], in_=xs[:, k, :], func=AF.Identity,
                                               bias=modT[:, k:k + 1], scale=A1[:, k:k + 1])
                        return ins

                    o_h = P.add("act", fn_h, xs_res[i].rd() + [opA1] + hT_res.wr(), keep=True)
                    xs_res[i].did_read(o_h)
                    hT_res.did_write(o_h, fresh=(t == 0))

                norm_stage1(0)
                norm_stage1(1)
                mms = [mod_tile(wm, wm_res, jt % 2, jt * 512, 512) for jt in range(8)]
                fin1 = mod_finish(0, 32, [mms[-1]])
                opA1 = P.add("dve", lambda e: e.scalar_tensor_tensor(out=A1[:], in0=modT[:, 16:32], scalar=1.0,
                                                                    in1=gv[:, 0, :], op0=ALU.add, op1=ALU.mult),
                             [fin1, ld_gv], keep=True)
                norm_stage2(0, opA1)
                norm_stage2(1, opA1)
                norm_stage1(2)
                norm_stage1(3)
                norm_stage2(2, opA1)
                norm_stage2(3, opA1)
                if "d_hT" in dbg:
                    dump("d_hT", hT[:], hT_res.rd())
                    dump("d_modT", modT[:], [fin1])
                P.emit_block()
            if stop == "N":
                fin = P.add("sp", None, final_deps)
                P.emit_block()
                return nc

            CQ = sb(sc_h, "CQ", [128, S], BF16)
            Ctok = sb(sc_h, "Ctok", [128, 128], F32)
            sel_bf = sb(sc_h, "sel_bf", [128, 8, 128], BF16)
            maskc_bf = sb(sc_h, "maskc_bf", [128, 128], BF16)
            ld_sel = dma("sp", sel_bf[:], k_sel, [], "ld_small0")
            ld_mc = dma("sp", maskc_bf[:], k_maskc, [], "ld_small1")

            with ExitStack() as sc_f:
                wg3 = sb(sc_f, "wg3", [128, KC, 72], BF16)
                b72 = sb(sc_f, "b72", [72, 1], F32)
                nb72 = sb(sc_f, "nb72", [72, 1], F32)
                Cf = sb(sc_f, "Cf", [72, S], F32)
                lf = sb(sc_f, "lf", [72, S], F32)
                onesF = sb(sc_f, "onesF", [72, S], F32)
                Thi = sb(sc_f, "Thi", [72, S], BF16)
                identF = sb(sc_f, "identF", [8, 8], F32)
                ld_if = dma("sp", identF[:], k_identf, [], "ld_small2")
                ms_w = P.add("dve", lambda e: e.memset(wg3[:], 0.0), [])
                ms_b = P.add("dve", lambda e: e.memset(b72[:], 0.0), [])
                ms_1 = P.add("dve", lambda e: e.memset(onesF[:], 1.0), [])
                ms_q = P.add("dve", lambda e: e.memset(CQ[:], 0.0), [])
                ldw = []
                ldb = []
                for i, r0 in enumerate((0, 32, 64)):
                    ldw.append(dma("pool", wg3[:, :, r0:r0 + 8], w_in_v[:, :, C_FG:C_FG + 8], [ms_w], f"wg{i}"))
                    ldb.append(dma("sp", b72[r0:r0 + 8, :], b_fg, [ms_b], f"bg{i}"))
                o_nb = P.add("dve", lambda e: e.tensor_scalar(out=nb72[:], in0=b72[:], scalar1=-1.0, scalar2=None,
                                                              op0=ALU.mult), ldb)
                lf_w = []
                for t in range(NT):
                    bk = t % 2

                    def fn_fg(e, t=t, bk=bk):
                        ins = None
                        for k in range(KC):
                            ins = e.matmul(banks[bk][0:72, :], lhsT=wg3[:, k, :], rhs=hT[:, k, t * TS:(t + 1) * TS],
                                           start=(k == 0), stop=(k == KC - 1))
                        return ins

                    o_fg = P.add("pe", fn_fg, ldw + hT_res.rd() + bres[bk].wr())
                    bres[bk].did_write(o_fg)
                    o_e = P.add("act", lambda e, t=t, bk=bk: e.activation(
                        out=lf[:, t * TS:(t + 1) * TS], in_=banks[bk][0:72, :], func=AF.Exp, bias=nb72[:, 0:1], scale=-1.0),
                        bres[bk].rd() + [o_nb])
                    bres[bk].did_read(o_e)
                    o_l = P.add("act", lambda e, t=t: e.activation(
                        out=lf[:, t * TS:(t + 1) * TS], in_=lf[:, t * TS:(t + 1) * TS], func=AF.Ln, bias=1.0, scale=1.0),
                        [o_e])
                    lf_w.append(o_l)
                o_scan = P.add("dve", lambda e: e.tensor_tensor_scan(out=Cf[:], data0=onesF[:], data1=lf[:], initial=0.0,
                                                                     op0=ALU.mult, op1=ALU.add), lf_w + [ms_1])
                o_neg = P.add("dve", lambda e: e.tensor_scalar(out=lf[:], in0=Cf[:], scalar1=-1.0, scalar2=None,
                                                               op0=ALU.mult), [o_scan])
                o_hi = P.add("dve", lambda e: e.tensor_copy(out=Thi[:], in_=lf[:]), [o_neg])
                o_r1 = P.add("dve", lambda e: e.tensor_tensor(out=onesF[:], in0=lf[:], in1=Thi[:], op=ALU.subtract), [o_hi])
                o_mid = P.add("dve", lambda e: e.tensor_copy(out=CQ[0:72, :], in_=onesF[:]), [o_r1, ms_q])
                o_r2 = P.add("dve", lambda e: e.tensor_tensor(out=lf[:], in0=onesF[:], in1=CQ[0:72, :], op=ALU.subtract), [o_mid])
                o_lo = P.add("dve", lambda e: e.tensor_copy(out=CQ[64:72, :], in_=lf[64:72, :]), [o_r2])
                o_hi2 = P.add("dve", lambda e: e.tensor_copy(out=CQ[0:32, :], in_=Thi[0:32, :]), [o_r2], keep=True)
                cq_ready = [o_lo, o_hi2]
                o_lo.keep = True

                def fn_tr(e):
                    ins = None
                    for blk in range(16):
                        ins = e.transpose(out=banks[2][:, blk * 8:(blk + 1) * 8], in_=Cf[0:8, blk * 128:(blk + 1) * 128],
                                          identity=identF[:])
                    return ins

                o_tr = P.add("pe", fn_tr, [o_scan, ld_if] + bres[2].wr())
                bres[2].did_write(o_tr)
                o_ct = P.add("dve", lambda e: e.tensor_copy(out=Ctok[:], in_=banks[2][:, 0:128]), [o_tr], keep=True)
                bres[2].did_read(o_ct)
                if "d_C" in dbg:
                    dump("d_C", Cf[:], [o_scan])
                    dump("d_Ctok", Ctok[:], [o_ct])
                    dump("d_CQ", CQ[:], cq_ready)
                P.emit_block()
            if stop == "FG":
                P.add("sp", None, final_deps)
                P.emit_block()
                return nc

            pbank_ctr = [0]

            def next_pbank():
                b = pbank_ctr[0] % 3
                pbank_ctr[0] += 1
                return b

            def proj_fm(wt, wres, t, evac_fn, evac_eng, dst_res, fresh):
                bk = next_pbank()

                def fn(e, bk=bk):
                    ins = None
                    for k in range(KC):
                        ins = e.matmul(banks[bk][:, :], lhsT=wt[:, k, :], rhs=hT[:, k, t * TS:(t + 1) * TS],
                                       start=(k == 0), stop=(k == KC - 1))
                    return ins

                o = P.add("pe", fn, wres.rd() + hT_res.rd() + bres[bk].wr())
                wres.did_read(o)
                bres[bk].did_write(o)
                ev = P.add(evac_eng, lambda e, bk=bk: evac_fn(e, banks[bk]), bres[bk].rd() + dst_res.wr() if fresh else bres[bk].rd() + dst_res.rs)
                bres[bk].did_read(ev)
                dst_res.did_write(ev, fresh=fresh)
                return o, ev, bk

            def proj_v_tok(wt, wres, blk4, vt, vres, fresh):
                bk = next_pbank()

                def fn(e, bk=bk):
                    ins = None
                    for bi in range(4):
                        blk = blk4 * 4 + bi
                        for k in range(KC):
                            ins = e.matmul(banks[bk][:, bi * 128:(bi + 1) * 128], lhsT=hT[:, k, blk * 128:(blk + 1) * 128],
                                           rhs=wt[:, k, :], start=(k == 0), stop=(k == KC - 1))
                    return ins

                o = P.add("pe", fn, wres.rd() + hT_res.rd() + bres[bk].wr())
                wres.did_read(o)
                bres[bk].did_write(o)
                ev = P.add("dve", lambda e, bk=bk: e.tensor_copy(
                    out=vt[:, blk4 * 4:(blk4 + 1) * 4, :], in_=banks[bk][:, :].rearrange("p (a d) -> p a d", a=4)),
                    bres[bk].rd() + (vres.wr() if fresh else vres.rs))
                bres[bk].did_read(ev)
                vres.did_write(ev, fresh=fresh)
                return ev

            with ExitStack() as sc_x:
                wm2 = [sb(sc_x, f"wmX{i}", [128, KC, 128], BF16) for i in range(2)]
                wm2_res = [Res(), Res()]
                wq = [sb(sc_x, f"wq{i}", [128, KC, 128], BF16) for i in range(2)]
                wk = [sb(sc_x, f"wk{i}", [128, KC, 128], BF16) for i in range(2)]
                wv = [sb(sc_x, f"wv{i}", [128, KC, 128], BF16) for i in range(2)]
                wq_res = [Res(), Res()]
                wk_res = [Res(), Res()]
                wv_res = [Res(), Res()]
                qT = [sb(sc_x, f"qT{i}", [128, S], BF16) for i in range(2)]
                kT = [sb(sc_x, f"kT{i}", [128, S], BF16) for i in range(2)]
                vtok = [sb(sc_x, f"vtok{i}", [128, 16, 128], BF16) for i in range(2)]
                q_res = [Res(), Res()]
                k_res = [Res(), Res()]
                v_res = [Res(), Res()]
                NPB = 4
                Pb = [sb(sc_x, f"Pb{i}", [128, TS], BF16) for i in range(NPB)]
                Pb_res = [Res() for _ in range(NPB)]
                rden = [sb(sc_x, f"rden{i}", [128, TS], F32) for i in range(2)]
                rden_res = [Res(), Res()]
                CT["pb_ctr"] = 0
                CT["sbank_ctr"] = 0
                mod_jobs = [(32 * 128 + i * 128) for i in range(64)]
                mod_ops = []
                CT["mod_i"] = 0
                CT["jt_ctr"] = 0
                def load_head(hh):
                    ss = hh % 2
                    for (wt, wres, c0, nm) in ((wq, wq_res, C_FQ, "wq"), (wk, wk_res, C_FK, "wk"), (wv, wv_res, C_FV, "wv")):
                        ld = dma("pool", wt[ss][:], w_in_v[:, :, c0 + hh * 128:c0 + (hh + 1) * 128], wres[ss].wr(), f"{nm}{ss}")
                        wres[ss].did_write(ld)

                def mod_step():
                    i = CT["mod_i"]
                    if not OPTS["mod_bg"] or i >= len(mod_jobs):
                        return
                    mod_ops.append(mod_pe(wm2, wm2_res, i % 2, mod_jobs[i], 128))
                    if i + 2 < len(mod_jobs):
                        mod_dma(wm2, wm2_res, i % 2, mod_jobs[i + 2], 128)
                    CT["mod_i"] += 1

                if OPTS["nfox"] > 0:
                    load_head(0)
                if OPTS["mod_bg"]:
                    mod_dma(wm2, wm2_res, 0, mod_jobs[0], 128)
                    mod_dma(wm2, wm2_res, 1, mod_jobs[1], 128)

                def fox_head(h):
                    s = h % 2
                    for t in range(NT):
                        proj_fm(wq[s], wq_res[s], t,
                                lambda e, bank, t=t, s=s: e.activation(out=qT[s][:, t * TS:(t + 1) * TS], in_=bank[:, :],
                                                                        func=AF.Copy, scale=SCALE),
                                "act", q_res[s], fresh=(t == 0))
                        mod_step()
                    for t in range(NT):
                        proj_fm(wk[s], wk_res[s], t,
                                lambda e, bank, t=t, s=s: e.tensor_copy(out=kT[s][:, t * TS:(t + 1) * TS], in_=bank[:, :]),
                                "dve", k_res[s], fresh=(t == 0))
                        mod_step()
                    if h + 1 < OPTS["nfox"]:
                        load_head(h + 1)
                    for b4 in range(4):
                        proj_v_tok(wv[s], wv_res[s], b4, vtok[s], v_res[s], fresh=(b4 == 0))
                    issue_conv(5)
                    def fox_qtile(j):
                        ob = 3 + 2 * (CT["jt_ctr"] % 2)
                        db = ob + 1
                        rd_i = CT["jt_ctr"] % 2
                        CT["jt_ctr"] += 1
                        nkb = 4 * j + 4
                        qk_ops = {}
                        ex_ops = {}

                        def make_qk(kb):
                            sbk = CT["sbank_ctr"] % 3
                            CT["sbank_ctr"] += 1
                            c0 = max(0, kb - 4 * j) * 128

                            def fn(e, kb=kb, sbk=sbk, c0=c0):
                                e.matmul(banks[sbk][:, c0:TS], lhsT=kT[s][:, kb * 128:(kb + 1) * 128],
                                         rhs=qT[s][:, j * TS + c0:(j + 1) * TS], start=True, stop=False)
                                diag = kb >= 4 * j
                                ins = e.matmul(banks[sbk][:, c0:TS], lhsT=sel_bf[0:72, h, :],
                                               rhs=CQ[0:72, j * TS + c0:(j + 1) * TS], start=False, stop=not diag)
                                if diag:
                                    ins = e.matmul(banks[sbk][:, c0:c0 + 128], lhsT=ident_bf[:], rhs=maskc_bf[:],
                                                   start=False, stop=True)
                                return ins

                            o = P.add("pe", fn, k_res[s].rd() + q_res[s].rd() + cq_ready + [ld_sel, ld_mc, ld_id] + bres[sbk].wr())
                            k_res[s].did_read(o)
                            q_res[s].did_read(o)
                            bres[sbk].did_write(o)
                            qk_ops[kb] = (o, sbk, c0)

                        def make_exp(kb):
                            o_qk, sbk, c0 = qk_ops[kb]
                            pi = CT["pb_ctr"] % NPB
                            CT["pb_ctr"] += 1
                            o = P.add("act", lambda e, sbk=sbk, c0=c0, pi=pi, kb=kb: e.activation(
                                out=Pb[pi][:, c0:TS], in_=banks[sbk][:, c0:TS], func=AF.Exp,
                                bias=Ctok[:, kb * 8 + h:kb * 8 + h + 1], scale=1.0),
                                bres[sbk].rd() + Pb_res[pi].wr() + [o_ct])
                            bres[sbk].did_read(o)
                            Pb_res[pi].did_write(o)
                            ex_ops[kb] = (o, pi, c0)

                        def make_pv(kb):
                            o_ex, pi, c0 = ex_ops[kb]

                            def fn(e, kb=kb, pi=pi, c0=c0):
                                e.matmul(banks[ob][:, c0:TS], lhsT=vtok[s][:, kb, :], rhs=Pb[pi][:, c0:TS],
                                         start=(kb == 0), stop=(kb == nkb - 1))
                                return e.matmul(banks[db][:, c0:TS], lhsT=ones_bf[:], rhs=Pb[pi][:, c0:TS],
                                                start=(kb == 0), stop=(kb == nkb - 1))

                            deps = Pb_res[pi].rd() + v_res[s].rd()
                            if kb == 0:
                                deps = deps + bres[ob].wr() + bres[db].wr()
                            o = P.add("pe", fn, deps)
                            Pb_res[pi].did_read(o)
                            v_res[s].did_read(o)
                            bres[ob].did_write(o, fresh=(kb == 0))
                            bres[db].did_write(o, fresh=(kb == 0))
                            return o

                        make_qk(0)
                        if nkb > 1:
                            make_qk(1)
                        last_pv = None
                        for kb in range(nkb):
                            make_exp(kb)
                            if kb + 2 < nkb:
                                make_qk(kb + 2)
                            last_pv = make_pv(kb)
                        rd = rden[rd_i]
                        o_rc = P.add("dve", lambda e, rd=rd, db=db: e.reciprocal(out=rd[:], in_=banks[db][:, :]),
                                     [last_pv] + rden_res[rd_i].wr())
                        bres[db].did_read(o_rc)
                        rden_res[rd_i].did_write(o_rc)
                        o_mx = P.add("dve", lambda e, rd=rd, ob=ob, j=j: e.tensor_tensor(
                            out=mix_v[:, h, j * TS:(j + 1) * TS], in0=banks[ob][:, :], in1=rd[:], op=ALU.mult),
                            [last_pv, o_rc] + mix_res.rs, keep=True)
                        bres[ob].did_read(o_mx)
                        rden_res[rd_i].did_read(o_mx)
                        mix_res.did_write(o_mx, fresh=False)
                    for j in range(NT):
                        fox_qtile(j)

                for h in range(OPTS["nfox"]):
                    fox_head(h)
                fin2 = mod_finish(32, 32 + CT["mod_i"], mod_ops[-1:]) if CT["mod_i"] > 0 else None
                oG1 = oA2 = oG2 = None
                if fin2 is not None:
                    oG1 = P.add("dve", lambda e: e.tensor_tensor(out=G1[:], in0=modT[:, 32:48], in1=gv[:, 1, :], op=ALU.mult), [fin2], keep=True)
                    oA2 = P.add("dve", lambda e: e.scalar_tensor_tensor(out=A2[:], in0=modT[:, 64:80], scalar=1.0, in1=gv[:, 2, :],
                                                                        op0=ALU.add, op1=ALU.mult), [fin2], keep=True)
                    oG2 = P.add("dve", lambda e: e.tensor_tensor(out=G2[:], in0=modT[:, 80:96], in1=gv[:, 3, :], op=ALU.mult), [fin2], keep=True)
                P.emit_block()

            with ExitStack() as sc_s:
                wrot = [sb(sc_s, f"wrot{i}", [128, KC, 128], BF16) for i in range(2)]
                wrot_res = [Res(), Res()]
                wsv = sb(sc_s, "wsv", [128, KC, 128], BF16)
                wsv_res = Res()
                sqT = sb(sc_s, "sqT", [128, 4, S], BF16)
                skT = sb(sc_s, "skT", [128, S], BF16)
                vts = sb(sc_s, "vts", [128, 16, 128], BF16)
                sq_res = Res()
                sk_res = Res()
                vs_res = Res()
                cs = [sb(sc_s, f"cs{i}", [128, TS], F32) for i in range(2)]
                sn = [sb(sc_s, f"sn{i}", [128, TS], F32) for i in range(2)]
                cs_res = [Res(), Res()]
                qb = [sb(sc_s, f"qb{i}", [128, TS], BF16) for i in range(2)]
                qb_res = [Res(), Res()]
                t1 = [sb(sc_s, f"t1_{i}", [128, TS], F32) for i in range(1)]
                t2 = [sb(sc_s, f"t2_{i}", [128, TS], F32) for i in range(1)]
                t_res = [Res()]
                NPS = 6
                Ps = [sb(sc_s, f"Ps{i}", [128, TS], BF16) for i in range(NPS)]
                Ps_res = [Res() for _ in range(NPS)]
                perm_bf = sb(sc_s, "perm_bf", [128, 128], BF16)
                mswa_bf = sb(sc_s, "mswa_bf", [128, 512], BF16)
                sinks_sb = sb(sc_s, "sinks_sb", [128, 8], F32)
                esink = sb(sc_s, "esink", [128, 8], F32)
                esinkB = sb(sc_s, "esinkB", [128, 8, 128], F32)
                rds = [sb(sc_s, f"rds{i}", [128, TS], F32) for i in range(2)]
                rds_res = [Res(), Res()]
                ld_pm = dma("sp", perm_bf[:], k_perm, [], "ld_small0")
                ld_ms = dma("sp", mswa_bf[:], k_maskswa, [], "ld_small1")
                ld_sk = dma("sp", sinks_sb[:], sinksB, [], "ld_small2")
                o_es = P.add("act", lambda e: e.activation(out=esink[:], in_=sinks_sb[:], func=AF.Exp), [ld_sk])

                def fn_esb(e):
                    ins = None
                    for hh in range(8):
                        ins = e.activation(out=esinkB[:, hh, :], in_=ident_bf[:], func=AF.Identity,
                                           bias=esink[:, hh:hh + 1], scale=0.0)
                    return ins

                o_esb = P.add("act", fn_esb, [o_es, ld_id], keep=True)
                CT["ps_ctr"] = 0
                CT["rp_ctr"] = 0
                CT["qblk_ctr"] = 0
                CT["wrot_ctr"] = 0
                CT["cs_ctr"] = 0
                def swa_group(g):
                    ld = dma("pool", wsv[:], w_in_v[:, :, C_SV + g * 128:C_SV + (g + 1) * 128], wsv_res.wr(), "wsv")
                    wsv_res.did_write(ld)
                    def load_rot(gg, pp):
                        wi = (gg * 5 + pp) % 2
                        if pp < 4:
                            c0w = C_SQ + (4 * gg + pp) * 128
                        else:
                            c0w = C_SK + gg * 128
                        ld = dma("pool", wrot[wi][:], w_in_v[:, :, c0w:c0w + 128], wrot_res[wi].wr(), f"wrot{wi}")
                        wrot_res[wi].did_write(ld)

                    if g == 0:
                        load_rot(0, 0)
                    rope_banks = [0, 1, 2, 7]
                    pend_rope = []

                    def rope_stage1(pi5, t, wt, wres):
                        ci = CT["cs_ctr"] % 2
                        CT["cs_ctr"] += 1
                        ldc = dma("sp", cs[ci][:], k_cos[:, t * TS:(t + 1) * TS], cs_res[ci].wr(), f"xs{ci}")
                        lds = dma("sp", sn[ci][:], k_sin[:, t * TS:(t + 1) * TS], cs_res[ci].wr(), f"wm{ci}")
                        cs_res[ci].did_write(ldc)
                        cs_res[ci].did_write(lds, fresh=False)
                        ri = ci
                        bkA = rope_banks[CT["rp_ctr"] % 4]
                        CT["rp_ctr"] += 1

                        def fn_p(e, wt=wt, bkA=bkA, t=t):
                            ins = None
                            for k in range(KC):
                                ins = e.matmul(banks[bkA][:, :], lhsT=wt[:, k, :], rhs=hT[:, k, t * TS:(t + 1) * TS],
                                               start=(k == 0), stop=(k == KC - 1))
                            return ins

                        o_p = P.add("pe", fn_p, wres.rd() + hT_res.rd() + bres[bkA].wr())
                        wres.did_read(o_p)
                        bres[bkA].did_write(o_p)
                        o_qb = P.add("act", lambda e, ri=ri, bkA=bkA: e.activation(out=qb[ri][:], in_=banks[bkA][:, :], func=AF.Copy),
                                     bres[bkA].rd() + qb_res[ri].wr())
                        bres[bkA].did_read(o_qb)
                        qb_res[ri].did_write(o_qb)
                        return (pi5, t, ci, ri, bkA, o_qb)

                    def rope_stage2(st):
                        pi5, t, ci, ri, bkA, o_qb = st
                        bkB = rope_banks[CT["rp_ctr"] % 4]
                        CT["rp_ctr"] += 1
                        o_pm = P.add("pe", lambda e, ri=ri, bkB=bkB: e.matmul(banks[bkB][:, :], lhsT=perm_bf[:], rhs=qb[ri][:],
                                                                                start=True, stop=True),
                                     qb_res[ri].rd() + bres[bkB].wr() + [ld_pm])
                        qb_res[ri].did_read(o_pm)
                        bres[bkB].did_write(o_pm)
                        o_t1 = P.add("dve", lambda e, bkA=bkA, ci=ci: e.tensor_tensor(
                            out=t1[0][:], in0=banks[bkA][:, :], in1=cs[ci][:], op=ALU.mult),
                            bres[bkA].rd() + cs_res[ci].rd() + t_res[0].wr() + [o_qb])
                        bres[bkA].did_read(o_t1)
                        cs_res[ci].did_read(o_t1)
                        o_t2 = P.add("dve", lambda e, bkB=bkB, ci=ci: e.tensor_tensor(
                            out=t2[0][:], in0=banks[bkB][:, :], in1=sn[ci][:], op=ALU.mult),
                            bres[bkB].rd() + cs_res[ci].rd() + t_res[0].wr())
                        bres[bkB].did_read(o_t2)
                        cs_res[ci].did_read(o_t2)
                        t_res[0].did_write(o_t1)
                        t_res[0].did_write(o_t2, fresh=False)
                        if pi5 < 4:
                            dst = sqT[:, pi5, t * TS:(t + 1) * TS]
                            dres = sq_res
                        else:
                            dst = skT[:, t * TS:(t + 1) * TS]
                            dres = sk_res
                        fresh = (t == 0 and pi5 in (0, 4))
                        o_ad = P.add("dve", lambda e, dst=dst: e.tensor_tensor(out=dst, in0=t1[0][:], in1=t2[0][:], op=ALU.add),
                                     t_res[0].rd() + (dres.wr() if fresh else dres.rs))
                        t_res[0].did_read(o_ad)
                        dres.did_write(o_ad, fresh=fresh)

                    for pi5 in range(5):
                        wi = (g * 5 + pi5) % 2
                        wt, wres = wrot[wi], wrot_res[wi]
                        nxt = g * 5 + pi5 + 1
                        if nxt < 5 * (OPTS["nswa"] if OPTS["swa"] else 0):
                            load_rot(nxt // 5, nxt % 5)
                        for t in range(NT):
                            pend_rope.append(rope_stage1(pi5, t, wt, wres))
                            if len(pend_rope) > 1:
                                rope_stage2(pend_rope.pop(0))
                    while pend_rope:
                        rope_stage2(pend_rope.pop(0))
                    for t in range(NT):
                        proj_v_tok(wsv, wsv_res, t, vts, vs_res, fresh=(t == 0))
                    if OPTS["swa_lvl"] < 2:
                        return
                    Pinfo = {}

                    def make_sqk(kb, pair):
                        sbk = next_pbank()
                        W = 256 if kb < 15 else 128

                        def fn(e, kb=kb, pair=pair, sbk=sbk, W=W):
                            outv = banks[sbk][:, 0:2 * W].rearrange("p (a w) -> p a w", a=2)
                            e.matmul(outv, lhsT=skT[:, kb * 128:(kb + 1) * 128],
                                     rhs=sqT[:, 2 * pair:2 * pair + 2, kb * 128:kb * 128 + W], start=True, stop=False)
                            return e.matmul(outv, lhsT=ident_bf[:],
                                            rhs=mswa_bf[:].rearrange("p (a w) -> p a w", a=2)[:, :, 0:W], start=False, stop=True)

                        o = P.add("pe", fn, sk_res.rd() + sq_res.rd() + bres[sbk].wr() + [ld_ms, ld_id])
                        sk_res.did_read(o)
                        sq_res.did_read(o)
                        bres[sbk].did_write(o)
                        return o, sbk, W

                    def make_sexp(kb, pair, qk):
                        o_qk, sbk, W = qk
                        pi = CT["ps_ctr"] % NPS
                        CT["ps_ctr"] += 1
                        o = P.add("act", lambda e, sbk=sbk, W=W, pi=pi: e.activation(
                            out=Ps[pi][:, 0:2 * W], in_=banks[sbk][:, 0:2 * W], func=AF.Exp, scale=SCALE),
                            bres[sbk].rd() + Ps_res[pi].wr())
                        bres[sbk].did_read(o)
                        Ps_res[pi].did_write(o)
                        Pinfo[(kb, pair)] = (pi, W)

                    def make_spv(qbk):
                        ob = 3 + 2 * (CT["qblk_ctr"] % 2)
                        db = ob + 1
                        ri = CT["qblk_ctr"] % 2
                        CT["qblk_ctr"] += 1

                        def fn(e, qbk=qbk, ob=ob, db=db):
                            ins = None
                            for (bank_i, is_den) in ((ob, False), (db, True)):
                                for pair in range(2):
                                    outv = banks[bank_i][:, pair * 256:(pair + 1) * 256].rearrange("p (a w) -> p a w", a=2)
                                    first = True
                                    if qbk > 0:
                                        pi, W = Pinfo[(qbk - 1, pair)]
                                        rhs = Ps[pi][:, 0:2 * W].rearrange("p (a w) -> p a w", a=2)[:, :, 128:256]
                                        lhsT = ones_bf[:] if is_den else vts[:, qbk - 1, :]
                                        ins = e.matmul(outv, lhsT=lhsT, rhs=rhs, start=True, stop=False)
                                        first = False
                                    pi, W = Pinfo[(qbk, pair)]
                                    rhs = Ps[pi][:, 0:2 * W].rearrange("p (a w) -> p a w", a=2)[:, :, 0:128]
                                    lhsT = ones_bf[:] if is_den else vts[:, qbk, :]
                                    ins = e.matmul(outv, lhsT=lhsT, rhs=rhs, start=first, stop=True)
                            return ins

                        deps = vs_res.rd() + bres[ob].wr() + bres[db].wr()
                        used = []
                        for pair in range(2):
                            for kk in ((qbk - 1, qbk) if qbk > 0 else (qbk,)):
                                pi, W = Pinfo[(kk, pair)]
                                deps = deps + Ps_res[pi].rd()
                                used.append(pi)
                        o = P.add("pe", fn, deps)
                        for pi in used:
                            Ps_res[pi].did_read(o)
                        vs_res.did_read(o)
                        bres[ob].did_write(o)
                        bres[db].did_write(o)
                        rd = rds[ri]
                        o_a = P.add("dve", lambda e, rd=rd, db=db: e.tensor_tensor(
                            out=rd[:], in0=banks[db][:, :], in1=esinkB[:, 4 * g:4 * g + 4, :].rearrange("p a w -> p (a w)"), op=ALU.add),
                            [o, o_esb] + rds_res[ri].wr())
                        bres[db].did_read(o_a)
                        o_ln = P.add("act", lambda e, rd=rd: e.activation(out=rd[:], in_=rd[:], func=AF.Ln), [o_a])
                        o_r = P.add("act", lambda e, rd=rd: e.activation(out=rd[:], in_=rd[:], func=AF.Exp, scale=-1.0), [o_ln])
                        rds_res[ri].did_write(o_r)
                        o_m = P.add("dve", lambda e, rd=rd, ob=ob, qbk=qbk: e.tensor_tensor(
                            out=mix_v[:, 8 + 4 * g:8 + 4 * g + 4, qbk * 128:(qbk + 1) * 128],
                            in0=banks[ob][:, :].rearrange("p (a w) -> p a w", a=4),
                            in1=rd[:].rearrange("p (a w) -> p a w", a=4), op=ALU.mult),
                            [o, o_r] + mix_res.rs, keep=True)
                        bres[ob].did_read(o_m)
                        rds_res[ri].did_read(o_m)
                        mix_res.did_write(o_m, fresh=False)

                    pend = [make_sqk(0, 0), make_sqk(0, 1)]
                    for kb in range(16):
                        nxt = []
                        if kb + 1 < 16:
                            nxt = [make_sqk(kb + 1, 0), make_sqk(kb + 1, 1)] if False else []
                        make_sexp(kb, 0, pend[0])
                        make_sexp(kb, 1, pend[1])
                        if kb + 1 < 16:
                            pend_next = [make_sqk(kb + 1, 0), make_sqk(kb + 1, 1)]
                        if OPTS["swa_lvl"] >= 3:
                            make_spv(kb)
                        if kb + 1 < 16:
                            pend = pend_next
                for g in range(OPTS["nswa"] if OPTS["swa"] else 0):
                    swa_group(g)
                if "d_mix" in dbg:
                    dump("d_mix", mixT[:], mix_res.rd())
                P.emit_block()
        if stop == "A":
            P.add("sp", None, final_deps)
            P.emit_block()
            return nc

        with ExitStack() as sc_2:
            ya = sb(sc_2, "ya", [128, KC, TS], F32)
            x1 = sb(sc_2, "x1", [128, KC, TS], F32)
            h2 = sb(sc_2, "h2", [128, KC, TS], BF16)
            abuf = [sb(sc_2, f"abuf{i}", [128, 8, TS], BF16) for i in range(2)]
            NSL = 3
            ring = [sb(sc_2, f"ring{i}", [128, 4096], BF16) for i in range(NSL)]
            ring_res = [Res() for _ in range(NSL)]
            NSQ = 4
            sqc = [sb(sc_2, f"sqc{i}", [128, TS], BF16) for i in range(NSQ)]
            sqc_res = [Res() for _ in range(NSQ)]
            rt = [sb(sc_2, f"rt{i}", [128, TS], F32) for i in range(2)]
            rt_res = [Res(), Res()]
            rtmp = sb(sc_2, "rtmp", [128, TS], F32)
            rtmp_res = Res()
            ya_c = [Res() for _ in range(KC)]
            x1_c = [Res() for _ in range(KC)]
            h2_res = Res()
            a_res = [Res(), Res()]
            CT["ring_ctr"] = 0
            CT["ob_ctr"] = 0
            CT["sq_ctr"] = 0
            CT["rt_ctr"] = 0
            SBK = [6, 7]
            stat_ctr = [0]
            pending = []

            def load_w(src_ap, view, cvk="A"):
                si = CT["ring_ctr"] % NSL
                CT["ring_ctr"] += 1
                if view == 16:
                    dst = ring[si][:].rearrange("p (k n) -> p k n", k=16)
                else:
                    dst = ring[si][:].rearrange("p (k n) -> p k n", k=8)
                ld = dma("sp", dst, src_ap, ring_res[si].wr() + cvdep[cvk], f"ring{si}")
                ring_res[si].did_write(ld)
                return si, dst

            def next_ob():
                b = CT["ob_ctr"] % 5
                CT["ob_ctr"] += 1
                return b

            def stats_add(src_ap, src_deps, sbk, first, last):
                qi = CT["sq_ctr"] % NSQ
                CT["sq_ctr"] += 1
                o_s = P.add("act", lambda e, qi=qi: e.activation(out=sqc[qi][:], in_=src_ap, func=AF.Square),
                            src_deps + sqc_res[qi].wr())
                sqc_res[qi].did_write(o_s)

                def mk(qi=qi, sbk=sbk, first=first, last=last):
                    o_m = P.add("pe", lambda e: e.matmul(banks[sbk][:, :], lhsT=ones_bf[:], rhs=sqc[qi][:], start=first, stop=last),
                                sqc_res[qi].rd() + (bres[sbk].wr() if first else []))
                    sqc_res[qi].did_read(o_m)
                    bres[sbk].did_write(o_m, fresh=first)
                    return o_m

                pending.append(mk)
                return o_s

            last_stat = [None]

            def flush_stats(keep):
                while len(pending) > keep:
                    last_stat[0] = pending.pop(0)()

            def make_rstd(sbk):
                flush_stats(0)
                o1 = P.add("act", lambda e: e.activation(out=rtmp[:], in_=banks[sbk][:, :], func=AF.Sqrt, bias=EPS, scale=1.0 / D),
                           [last_stat[0]] + bres[sbk].rd() + rtmp_res.wr())
                bres[sbk].did_read(o1)
                rtmp_res.did_write(o1)
                o2 = P.add("dve", lambda e: e.reciprocal(out=banks[sbk][:, :], in_=rtmp[:]), [o1] + bres[sbk].wr())
                rtmp_res.did_read(o2)
                bres[sbk].did_write(o2)
                return o2

            out_stores = []
            cvdep = {}

            def p2_tile(j):
                tsl = slice(j * TS, (j + 1) * TS)
                for q in range(4):
                    deps = []
                    for m in range(4 * q, 4 * q + 4):
                        deps += x1_c[m].wr()
                    ldx = dma("pool", x1[:, 4 * q:4 * q + 4, :], xT_v[:, 4 * q:4 * q + 4, tsl], deps, f"ldx{q}")
                    for m in range(4 * q, 4 * q + 4):
                        x1_c[m].did_write(ldx)
                if j == 0:
                    issue_conv(max(0, NCVA - len(conv_ops)))
                    cvdep["A"] = [conv_ops[min(NCVA, len(conv_ops)) - 1]]
                    issue_conv(len(conv_jobs))
                    cvdep["B"] = [conv_ops[-1]]
                sb1 = SBK[stat_ctr[0] % 2]
                stat_ctr[0] += 1
                for oc in range(8):
                    si, wv16 = load_w(wo_b[oc], 16)
                    for mm in range(2):
                        m = 2 * oc + mm
                        bk = next_ob()

                        def fn(e, wv16=wv16, mm=mm, bk=bk):
                            ins = None
                            for k in range(KC):
                                ins = e.matmul(banks[bk][:, :], lhsT=wv16[:, k, mm * 128:(mm + 1) * 128], rhs=mix_v[:, k, tsl],
                                               start=(k == 0), stop=(k == KC - 1))
                            return ins

                        o = P.add("pe", fn, ring_res[si].rd() + mix_res.rd() + bres[bk].wr())
                        ring_res[si].did_read(o)
                        mix_res.did_read(o)
                        bres[bk].did_write(o)
                        flush_stats(2)
                        ev = P.add("dve", lambda e, bk=bk, m=m: e.tensor_copy(out=ya[:, m, :], in_=banks[bk][:, :]),
                                   bres[bk].rd() + ya_c[m].wr())
                        bres[bk].did_read(ev)
                        ya_c[m].did_write(ev)
                        o_s = stats_add(ya[:, m, :], [ev], sb1, first=(m == 0), last=(m == KC - 1))
                        ya_c[m].did_read(o_s)
                o_rs = make_rstd(sb1)
                sb2 = SBK[stat_ctr[0] % 2]
                stat_ctr[0] += 1
                for m in range(KC):
                    tb = 5
                    o1 = P.add("dve", lambda e, m=m, tb=tb: e.scalar_tensor_tensor(out=banks[tb][:, :], in0=ya[:, m, :], scalar=G1[:, m:m + 1],
                                                                                   in1=banks[sb1][:, :], op0=ALU.mult, op1=ALU.mult),
                               ya_c[m].rd() + bres[tb].wr() + [o_rs, oG1])
                    bres[sb1].did_read(o1)
                    ya_c[m].did_read(o1)
                    bres[tb].did_write(o1)
                    o2 = P.add("dve", lambda e, m=m, tb=tb: e.tensor_tensor(out=x1[:, m, :], in0=banks[tb][:, :], in1=x1[:, m, :], op=ALU.add),
                               [o1] + x1_c[m].wr())
                    bres[tb].did_read(o2)
                    x1_c[m].did_write(o2)
                    o_s = stats_add(x1[:, m, :], [o2], sb2, first=(m == 0), last=(m == KC - 1))
                    x1_c[m].did_read(o_s)
                    flush_stats(0)
                o_rs2 = make_rstd(sb2)
                h2_ops = []
                for m in range(KC):
                    o3 = P.add("dve", lambda e, m=m: e.tensor_tensor(out=ya[:, m, :], in0=x1[:, m, :], in1=banks[sb2][:, :], op=ALU.mult),
                               ya_c[m].wr() + x1_c[m].rd() + [o_rs2])
                    bres[sb2].did_read(o3)
                    x1_c[m].did_read(o3)
                    ya_c[m].did_write(o3)
                    o4 = P.add("act", lambda e, m=m: e.activation(out=h2[:, m, :], in_=ya[:, m, :], func=AF.Identity,
                                                                  bias=modT[:, 48 + m:48 + m + 1], scale=A2[:, m:m + 1]),
                               [o3, oA2] + (h2_res.wr() if m == 0 else h2_res.rs))
                    ya_c[m].did_read(o4)
                    h2_res.did_write(o4, fresh=(m == 0))
                sb3 = SBK[stat_ctr[0] % 2]
                stat_ctr[0] += 1
                def mlp_up(g):
                    ai = g % 2
                    for ut in range(4):
                        si, wv16 = load_w(wu_b[g * 4 + ut], 16, "A" if g < 4 else "B")
                        for mm in range(2):
                            fc = 2 * ut + mm
                            bk = next_ob()

                            def fn(e, wv16=wv16, mm=mm, bk=bk):
                                ins = None
                                for k in range(KC):
                                    ins = e.matmul(banks[bk][:, :], lhsT=wv16[:, k, mm * 128:(mm + 1) * 128], rhs=h2[:, k, :],
                                                   start=(k == 0), stop=(k == KC - 1))
                                return ins

                            o = P.add("pe", fn, ring_res[si].rd() + h2_res.rd() + bres[bk].wr())
                            ring_res[si].did_read(o)
                            h2_res.did_read(o)
                            bres[bk].did_write(o)
                            flush_stats(2)
                            ri = CT["rt_ctr"] % 2
                            CT["rt_ctr"] += 1
                            o_r = P.add("act", lambda e, ri=ri, bk=bk: e.activation(out=rt[ri][:], in_=banks[bk][:, :], func=AF.Relu),
                                        bres[bk].rd() + rt_res[ri].wr())
                            bres[bk].did_read(o_r)
                            rt_res[ri].did_write(o_r)
                            o_a = P.add("dve", lambda e, ri=ri, bk=bk, ai=ai, fc=fc: e.scalar_tensor_tensor(
                                out=abuf[ai][:, fc, :], in0=banks[bk][:, :], scalar=0.0, in1=rt[ri][:], op0=ALU.max, op1=ALU.mult),
                                bres[bk].rd() + rt_res[ri].rd() + (a_res[ai].wr() if fc == 0 else a_res[ai].rs))
                            bres[bk].did_read(o_a)
                            rt_res[ri].did_read(o_a)
                            a_res[ai].did_write(o_a, fresh=(fc == 0))

                def mlp_down(g):
                    ai = g % 2
                    for dt in range(4):
                        si, wv8 = load_w(wd_b[g * 4 + dt], 8, "A" if g < 4 else "B")
                        for mm in range(4):
                            m = 4 * dt + mm
                            bk = next_ob()

                            def fn(e, wv8=wv8, mm=mm, bk=bk, ai=ai):
                                ins = None
                                for fc in range(8):
                                    ins = e.matmul(banks[bk][:, :], lhsT=wv8[:, fc, mm * 128:(mm + 1) * 128], rhs=abuf[ai][:, fc, :],
                                                   start=(fc == 0), stop=(fc == 7))
                                return ins

                            o = P.add("pe", fn, ring_res[si].rd() + a_res[ai].rd() + bres[bk].wr())
                            ring_res[si].did_read(o)
                            a_res[ai].did_read(o)
                            bres[bk].did_write(o)
                            flush_stats(2)
                            if g == 0:
                                ev = P.add("dve", lambda e, bk=bk, m=m: e.tensor_copy(out=ya[:, m, :], in_=banks[bk][:, :]),
                                           bres[bk].rd() + ya_c[m].wr())
                            else:
                                ev = P.add("dve", lambda e, bk=bk, m=m: e.tensor_tensor(out=ya[:, m, :], in0=banks[bk][:, :], in1=ya[:, m, :],
                                                                                         op=ALU.add),
                                           bres[bk].rd() + ya_c[m].wr())
                            ya_c[m].did_write(ev)
                            bres[bk].did_read(ev)
                            if g == 7:
                                o_s = stats_add(ya[:, m, :], [ev], sb3, first=(m == 0), last=(m == KC - 1))
                                ya_c[m].did_read(o_s)

                mlp_up(0)
                for g in range(8):
                    if g + 1 < 8:
                        mlp_up(g + 1)
                    mlp_down(g)
                o_rs3 = make_rstd(sb3)
                for q in range(4):
                    fin = []
                    for m in range(4 * q, 4 * q + 4):
                        tb = 5
                        o5 = P.add("dve", lambda e, m=m, tb=tb: e.scalar_tensor_tensor(out=banks[tb][:, :], in0=ya[:, m, :], scalar=G2[:, m:m + 1],
                                                                                       in1=banks[sb3][:, :], op0=ALU.mult, op1=ALU.mult),
                                   ya_c[m].rd() + bres[tb].wr() + [o_rs3, oG2])
                        bres[sb3].did_read(o5)
                        ya_c[m].did_read(o5)
                        bres[tb].did_write(o5)
                        o6 = P.add("dve", lambda e, m=m, tb=tb: e.tensor_tensor(out=x1[:, m, :], in0=banks[tb][:, :], in1=x1[:, m, :], op=ALU.add),
                                   [o5] + x1_c[m].wr())
                        bres[tb].did_read(o6)
                        x1_c[m].did_write(o6)
                        fin.append(o6)
                    st = dma("pool", outT_v[:, 4 * q:4 * q + 4, tsl], x1[:, 4 * q:4 * q + 4, :], fin, f"st{q}")
                    for m in range(4 * q, 4 * q + 4):
                        x1_c[m].did_read(st)
                    out_stores.append(st)

            for j in range(NT):
                p2_tile(j)
            P.add("sp", None, out_stores + final_deps)
            P.emit_block()
    return nc


def _host_inputs(inputs):
    f32 = np.float32
    x = np.asarray(inputs["x"], f32)
    c = np.asarray(inputs["c"], f32)
    consts = _consts()
    shared = {
        "w_mod": np.ascontiguousarray(np.asarray(inputs["w_mod"], f32)[0]),
        "b_modT": np.ascontiguousarray(np.asarray(inputs["b_mod"], f32)[0].reshape(96, 128).T),
        "gvec": np.ascontiguousarray(np.stack([np.asarray(inputs[k], f32)[0].reshape(KC, 128).T for k in
                                               ("g_pre_mix", "g_post_mix", "g_pre_mlp", "g_post_mlp")], axis=1)),
        "w_in": np.ascontiguousarray(np.asarray(inputs["w_in"], f32)[0]),
        "b_fg": np.ascontiguousarray(np.asarray(inputs["b_forget"], f32)[0].reshape(8, 1)),
        "sinksB": np.ascontiguousarray(np.broadcast_to(np.asarray(inputs["swa_sinks"], f32)[0][None, :], (128, 8))),
        "w_out": np.ascontiguousarray(np.asarray(inputs["w_out"], f32)[0]),
        "w_up": np.ascontiguousarray(np.asarray(inputs["w_up"], f32)[0]),
        "w_down": np.ascontiguousarray(np.asarray(inputs["w_down"], f32)[0]),
    }
    shared.update(consts)
    in_maps = []
    for b in range(8):
        m = dict(shared)
        m["xT"] = np.ascontiguousarray(x[b].T)
        m["c_col"] = np.ascontiguousarray(c[b].reshape(KC, 128).T)
        in_maps.append(m)
    return in_maps


def kernel(**inputs):
    in_maps = _host_inputs(inputs)
    nc = build("full")
    res = run_bass_kernel_spmd(nc, in_maps, core_ids=list(range(8)))
    out = np.stack([np.ascontiguousarray(r["outT"].T) for r in res.results], axis=0)
    return out.astype(np.float32)
```
